# Optimizing a Trainium2 kernel written in Bass

```python
import math
import jax, jax.numpy as jnp
from jax import lax
import numpy as np

D_MODEL = 1024
BATCH = 16
SEQ = 4096
DEPTH = 1

HGRN_HEADS = 4
HGRN_DK = 128
HGRN_DV = 128
HGRN_WIDTH = HGRN_HEADS * HGRN_DV
DIFF_HEADS = 4
DIFF_DH = 64
DIFF_DV = 2 * DIFF_DH
DIFF_WIDTH = DIFF_HEADS * DIFF_DV
MIX_WIDTH = HGRN_WIDTH + DIFF_WIDTH
IN_WIDTH = 4 * HGRN_WIDTH + 3 * DIFF_WIDTH
D_FF = -(-8 * D_MODEL // (3 * 256)) * 256
N_BUCKETS = 32
MAX_DISTANCE = 128
CHUNK = 64
Q_BLOCK = 128
LN_EPS = 1e-5
RMS_EPS = 1e-6

kernel_name = "hybrid_hgrn2_diffattn_deepnorm_adaln"


def _layer_norm(x, eps=LN_EPS):
    xf = x.astype(jnp.float32)
    mu = jnp.mean(xf, axis=-1, keepdims=True)
    var = jnp.mean(jnp.square(xf - mu), axis=-1, keepdims=True)
    return ((xf - mu) * lax.rsqrt(var + eps)).astype(x.dtype)


def _rms_norm(x, w, eps=RMS_EPS):
    xf = x.astype(jnp.float32)
    y = xf * lax.rsqrt(jnp.mean(jnp.square(xf), axis=-1, keepdims=True) + eps)
    return (y * w.astype(jnp.float32)).astype(x.dtype)


def _t5_bucket(dist):
    max_exact = N_BUCKETS // 2
    d = jnp.maximum(dist, 1).astype(jnp.float32)
    large = max_exact + (jnp.log(d / max_exact) / math.log(MAX_DISTANCE / max_exact)
                         * (N_BUCKETS - max_exact)).astype(jnp.int32)
    large = jnp.minimum(large, N_BUCKETS - 1)
    return jnp.where(dist < max_exact, dist, large)


def _hgrn2_chunkwise(q, f_pre, i, lb):
    B, S, H, DK = q.shape
    DV = i.shape[-1]
    nc = S // CHUNK

    def chunked(t):
        return t.astype(jnp.float32).reshape(B, nc, CHUNK, H, t.shape[-1]).transpose(0, 3, 1, 2, 4)

    fp = chunked(f_pre)
    qc = chunked(q) * (DK ** -0.5)
    vc = chunked(i)
    lbb = lb.astype(jnp.float32)[None, :, None, None, :]
    log_f = jnp.log(lbb + (1.0 - lbb) * jax.nn.sigmoid(fp))
    kc = (1.0 - lbb) * jax.nn.sigmoid(-fp)
    b = jnp.cumsum(log_f, axis=3)
    b_last = b[:, :, :, -1:, :]
    b_mid = b[:, :, :, CHUNK // 2 - 1:CHUNK // 2, :]
    a = jnp.einsum('bhntk,bhnsk->bhnts', qc * jnp.exp(b - b_mid), kc * jnp.exp(b_mid - b))
    causal = jnp.tril(jnp.ones((CHUNK, CHUNK), dtype=bool))
    a = jnp.where(causal, a, 0.0)
    o_intra = jnp.einsum('bhnts,bhnsv->bhntv', a, vc)
    q_in = qc * jnp.exp(b)
    k_st = kc * jnp.exp(b_last - b)
    d_last = jnp.exp(b_last[:, :, :, 0, :])

    def step(state, xs):
        qn, kn, vn, dn = xs
        o = jnp.einsum('bhtk,bhkv->bhtv', qn, state)
        state = dn[..., None] * state + jnp.einsum('bhtk,bhtv->bhkv', kn, vn)
        return state, o

    xs = (jnp.moveaxis(q_in, 2, 0), jnp.moveaxis(k_st, 2, 0),
          jnp.moveaxis(vc, 2, 0), jnp.moveaxis(d_last, 2, 0))
    s0 = jnp.zeros((B, H, DK, DV), jnp.float32)
    _, o_inter = lax.scan(step, s0, xs)
    o = o_intra + jnp.moveaxis(o_inter, 0, 2)
    return o.transpose(0, 2, 3, 1, 4).reshape(B, S, H, DV)


def _diff_attention(q, k, v, bias_table, lam):
    B, S, H = q.shape[0], q.shape[1], q.shape[2]
    nb = S // Q_BLOCK
    scale = DIFF_DH ** -0.5
    qb = q.reshape(B, nb, Q_BLOCK, H, 2, DIFF_DH).transpose(1, 0, 3, 4, 2, 5)
    kt = k.transpose(0, 2, 3, 1, 4)
    vt = v.transpose(0, 2, 1, 3)
    k_pos = jnp.arange(S, dtype=jnp.int32)

    def block(args):
        qblk, j = args
        q_pos = j * Q_BLOCK + jnp.arange(Q_BLOCK, dtype=jnp.int32)
        rel = q_pos[:, None] - k_pos[None, :]
        bias = bias_table[_t5_bucket(jnp.maximum(rel, 0))]
        bias = jnp.transpose(bias, (2, 0, 1)).astype(jnp.float32)
        s = jnp.einsum('bhmqd,bhmkd->bhmqk', qblk, kt,
                       preferred_element_type=jnp.float32) * scale
        s = s + bias[None, :, None]
        s = jnp.where((rel >= 0)[None, None, None], s, -jnp.inf)
        p = jax.nn.softmax(s, axis=-1)
        attn = p[:, :, 0] - lam * p[:, :, 1]
        return jnp.einsum('bhqk,bhkv->bhqv', attn.astype(v.dtype), vt)

    o = lax.map(block, (qb, jnp.arange(nb, dtype=jnp.int32)))
    return o.transpose(1, 0, 3, 2, 4).reshape(B, S, H, DIFF_DV)


def setup_inputs(seed: int = 0) -> dict:
    key = jax.random.key(seed)
    ks = jax.random.split(key, 24)
    beta = (8.0 * DEPTH) ** -0.25
    n = jax.random.normal
    f32 = jnp.float32
    return {
        "x": n(ks[0], (BATCH, SEQ, D_MODEL), f32),
        "c": n(ks[1], (BATCH, D_MODEL), f32),
        "w_ada": n(ks[2], (DEPTH, D_MODEL, 6 * D_MODEL), f32) * (0.5 * D_MODEL ** -0.5),
        "b_ada": n(ks[3], (DEPTH, 6 * D_MODEL), f32) * 0.01,
        "w_in": n(ks[4], (DEPTH, D_MODEL, IN_WIDTH), f32) * D_MODEL ** -0.5,
        "lb_logits": n(ks[5], (DEPTH + 1, HGRN_WIDTH), f32) * 0.1,
        "hgrn_norm_w": 1.0 + 0.01 * n(ks[6], (DEPTH, HGRN_DV), f32),
        "lam_q1": n(ks[7], (DEPTH, DIFF_DH), f32) * 0.1,
        "lam_k1": n(ks[8], (DEPTH, DIFF_DH), f32) * 0.1,
        "lam_q2": n(ks[9], (DEPTH, DIFF_DH), f32) * 0.1,
        "lam_k2": n(ks[10], (DEPTH, DIFF_DH), f32) * 0.1,
        "diff_norm_w": 1.0 + 0.01 * n(ks[11], (DEPTH, DIFF_DV), f32),
        "rel_bias": n(ks[12], (N_BUCKETS, DIFF_HEADS), f32) * 0.5,
        "w_out": n(ks[13], (DEPTH, MIX_WIDTH, D_MODEL), f32) * (beta * MIX_WIDTH ** -0.5),
        "ln1_g": 1.0 + 0.01 * n(ks[14], (DEPTH, D_MODEL), f32),
        "ln1_b": 0.01 * n(ks[15], (DEPTH, D_MODEL), f32),
        "w_gate": n(ks[16], (DEPTH, D_MODEL, D_FF), f32) * D_MODEL ** -0.5,
        "w_up": n(ks[17], (DEPTH, D_MODEL, D_FF), f32) * D_MODEL ** -0.5,
        "w_down": n(ks[18], (DEPTH, D_FF, D_MODEL), f32) * (beta * D_FF ** -0.5),
        "ln2_g": 1.0 + 0.01 * n(ks[19], (DEPTH, D_MODEL), f32),
        "ln2_b": 0.01 * n(ks[20], (DEPTH, D_MODEL), f32),
    }


def reference(x, c, w_ada, b_ada, w_in, lb_logits, hgrn_norm_w, lam_q1, lam_k1, lam_q2, lam_k2,
              diff_norm_w, rel_bias, w_out, ln1_g, ln1_b, w_gate, w_up, w_down, ln2_g, ln2_b):
    B, S, _ = x.shape
    alpha = (2.0 * DEPTH) ** 0.25
    lb_all = jnp.cumsum(jax.nn.softmax(lb_logits.astype(jnp.float32), axis=0), axis=0)
    splits = [HGRN_WIDTH, 2 * HGRN_WIDTH, 3 * HGRN_WIDTH, 4 * HGRN_WIDTH,
              4 * HGRN_WIDTH + DIFF_WIDTH, 4 * HGRN_WIDTH + 2 * DIFF_WIDTH]
    for l in range(DEPTH):
        ada = jax.nn.silu(c) @ w_ada[l] + b_ada[l]
        sh_m, sc_m, g_m, sh_f, sc_f, g_f = jnp.split(ada[:, None, :], 6, axis=-1)

        u = _layer_norm(x) * (1.0 + sc_m) + sh_m
        proj = u @ w_in[l]
        hq, hf, hi, hg, dq, dk, dv = jnp.split(proj, splits, axis=-1)

        lb = lb_all[l].reshape(HGRN_HEADS, HGRN_DK)
        o_h = _hgrn2_chunkwise(hq.reshape(B, S, HGRN_HEADS, HGRN_DK),
                               hf.reshape(B, S, HGRN_HEADS, HGRN_DK),
                               hi.reshape(B, S, HGRN_HEADS, HGRN_DV), lb).astype(x.dtype)
        o_h = _rms_norm(o_h, hgrn_norm_w[l]) * jax.nn.silu(hg.reshape(B, S, HGRN_HEADS, HGRN_DV))
        o_h = o_h.reshape(B, S, HGRN_WIDTH)

        lam_init = 0.8 - 0.6 * math.exp(-0.3 * l)
        lam = (jnp.exp(jnp.sum(lam_q1[l].astype(jnp.float32) * lam_k1[l].astype(jnp.float32)))
               - jnp.exp(jnp.sum(lam_q2[l].astype(jnp.float32) * lam_k2[l].astype(jnp.float32)))
               + lam_init)
        o_d = _diff_attention(dq.reshape(B, S, DIFF_HEADS, 2, DIFF_DH),
                              dk.reshape(B, S, DIFF_HEADS, 2, DIFF_DH),
                              dv.reshape(B, S, DIFF_HEADS, DIFF_DV), rel_bias, lam)
        o_d = (_rms_norm(o_d, diff_norm_w[l]) * (1.0 - lam_init)).reshape(B, S, DIFF_WIDTH)

        mix = jnp.concatenate([o_h, o_d], axis=-1) @ w_out[l]
        x = _layer_norm(alpha * x + (1.0 + g_m) * mix) * ln1_g[l] + ln1_b[l]

        u = _layer_norm(x) * (1.0 + sc_f) + sh_f
        y = (jax.nn.silu(u @ w_gate[l]) * (u @ w_up[l])) @ w_down[l]
        x = _layer_norm(alpha * x + (1.0 + g_f) * y) * ln2_g[l] + ln2_b[l]
    return x
```

```python
import math
import numpy as np
import concourse.bass as bass
import concourse.mybir as mybir
from concourse.bass_utils import run_bass_kernel_spmd

F32 = mybir.dt.float32
BF16 = mybir.dt.bfloat16
AF = mybir.ActivationFunctionType
ALU = mybir.AluOpType

D = 1024
DFF = 2816
NF = DFF // 128
INW = 3584
LN_EPS = 1e-5
RMS_EPS = 1e-6
ALPHA = 2.0 ** 0.25
LAM_INIT = 0.8 - 0.6 * math.exp(0.0)
MASKVAL = -240000.0


class Buf:
    __slots__ = ("name", "last_w", "readers")

    def __init__(self, name):
        self.name = name
        self.last_w = None
        self.readers = []


class T:
    __slots__ = ("ap", "bufs")

    def __init__(self, ap, bufs):
        self.ap = ap
        self.bufs = bufs if isinstance(bufs, (list, tuple)) else [bufs]

    def __getitem__(self, idx):
        return T(self.ap[idx], self.bufs)

    def bitcast(self, dt):
        return T(self.ap.bitcast(dt), self.bufs)

    def rearrange(self, pat, **kw):
        return T(self.ap.rearrange(pat, **kw), self.bufs)

    def bc(self, shape):
        return T(self.ap.broadcast_to(shape), self.bufs)

    def on(self, *bufs):
        return T(self.ap, list(bufs))


class Op:
    __slots__ = ("eng", "fn", "deps", "pos", "need_inc", "inc_val", "is_dma", "sem", "sem_val", "name")


ENGS = ["pe", "act", "dve", "pool", "sp"]
SAME_ENG_WIN = 8


class Sched:
    def __init__(self, nc):
        self.nc = nc
        self.ops = {e: [] for e in ENGS}
        self.barrier_deps = []
        self.dma_sem_of = {}
        self.dma_sem_cnt = []
        self.dma_last = []
        self.final_dma = []

    def add(self, eng, fn, reads, writes, dma_key=None, name=""):
        op = Op()
        op.eng = eng
        op.fn = fn
        op.name = name
        op.pos = len(self.ops[eng])
        op.need_inc = False
        op.inc_val = 0
        op.is_dma = dma_key is not None
        op.sem = None
        op.sem_val = 0
        deps = set(self.barrier_deps)
        rb = []
        for t in reads:
            if isinstance(t, T):
                rb.extend(t.bufs)
        wb = []
        for t in writes:
            wb.extend(t.bufs)
        for b in rb:
            if b.last_w is not None:
                deps.add(b.last_w)
        for b in wb:
            if b.last_w is not None:
                deps.add(b.last_w)
            deps.update(b.readers)
        op.deps = deps
        for b in rb:
            b.readers.append(op)
        for b in wb:
            b.last_w = op
            b.readers = []
        if op.is_dma:
            dma_key = (eng, dma_key)
            if dma_key not in self.dma_sem_of:
                self.dma_sem_of[dma_key] = len(self.dma_sem_cnt)
                self.dma_sem_cnt.append(0)
                self.dma_last.append(None)
            si = self.dma_sem_of[dma_key]
            self.dma_sem_cnt[si] += 1
            op.sem = si
            op.sem_val = 16 * self.dma_sem_cnt[si]
            self.dma_last[si] = op
        self.ops[eng].append(op)
        return op

    def barrier(self):
        deps = []
        for e in ENGS:
            if self.ops[e]:
                deps.append(self.ops[e][-1])
        for op in self.dma_last:
            if op is not None:
                deps.append(op)
        self.barrier_deps = deps

    @staticmethod
    def _a(x):
        return x.ap if isinstance(x, T) else x

    def mm(self, out, lhsT, rhs, start=True, stop=True):
        nc = self.nc
        return self.add("pe", lambda: nc.tensor.matmul(out.ap, lhsT.ap, rhs.ap, start=start, stop=stop),
                        [lhsT, rhs], [out], name="mm")

    def tr(self, out, in_, ident):
        nc = self.nc
        return self.add("pe", lambda: nc.tensor.transpose(out.ap, in_.ap, ident.ap), [in_, ident], [out], name="tr")

    def act(self, out, in_, func, scale=1.0, bias=None, eng="act"):
        nc = self.nc
        a = self._a
        kw = {}
        if bias is not None:
            kw["bias"] = a(bias)
        return self.add("act", lambda: nc.scalar.activation(out.ap, in_.ap, func, scale=a(scale), **kw),
                        [in_, scale, bias], [out], name="act")

    def _e(self, eng):
        return self.nc.vector if eng == "dve" else self.nc.gpsimd

    def ts(self, eng, out, in0, s1, op0, s2=None, op1=None):
        e = self._e(eng)
        a = self._a
        if op1 is None:
            fn = lambda: e.tensor_scalar(out.ap, in0.ap, a(s1), None, op0)
        else:
            fn = lambda: e.tensor_scalar(out.ap, in0.ap, a(s1), a(s2), op0, op1)
        return self.add(eng, fn, [in0, s1, s2], [out], name="ts")

    def tt(self, eng, out, in0, in1, op):
        e = self._e(eng)
        return self.add(eng, lambda: e.tensor_tensor(out.ap, in0.ap, in1.ap, op), [in0, in1], [out], name="tt")

    def stt(self, out, in0, scalar, in1, op0, op1):
        nc = self.nc
        a = self._a
        return self.add("dve", lambda: nc.vector.scalar_tensor_tensor(out.ap, in0.ap, a(scalar), in1.ap, op0, op1),
                        [in0, scalar, in1], [out], name="stt")

    def copy(self, eng, out, in_):
        if eng == "act":
            return self.act(out, in_, AF.Copy)
        e = self._e(eng)
        return self.add(eng, lambda: e.tensor_copy(out.ap, in_.ap), [in_], [out], name="copy")

    def recip(self, out, in_):
        nc = self.nc
        return self.add("dve", lambda: nc.vector.reciprocal(out.ap, in_.ap), [in_], [out], name="recip")

    def memset(self, eng, out, val):
        e = self._e(eng)
        return self.add(eng, lambda: e.memset(out.ap, val), [], [out], name="memset")

    def bn_stats(self, out, in_):
        nc = self.nc
        return self.add("dve", lambda: nc.vector.bn_stats(out.ap, in_.ap), [in_], [out], name="bns")

    def bn_aggr(self, out, in_):
        nc = self.nc
        return self.add("dve", lambda: nc.vector.bn_aggr(out.ap, in_.ap), [in_], [out], name="bna")

    def scan(self, out, d0, d1, init, op0, op1):
        nc = self.nc
        return self.add("dve", lambda: nc.vector.tensor_tensor_scan(out.ap, d0.ap, d1.ap, init, op0, op1),
                        [d0, d1], [out], name="scan")

    def dma(self, q, out, in_, key, final=False, **kw):
        nc = self.nc
        e = {"sp": nc.sync, "pool": nc.gpsimd, "act": nc.scalar}[q]
        op = self.add(q, lambda: e.dma_start(out=out.ap, in_=in_.ap, **kw), [in_], [out], dma_key=key, name="dma")
        if final:
            self.final_dma.append(op)
        return op

    def emit(self, block_engines, sems_compute, sems_dma):
        for e in ENGS:
            for op in self.ops[e]:
                for d in op.deps:
                    if d.is_dma:
                        continue
                    if d.eng != op.eng:
                        d.need_inc = True
                    elif op.eng != "pe" and (op.pos - d.pos) <= SAME_ENG_WIN:
                        d.need_inc = True
        for e in ENGS:
            cnt = 0
            for op in self.ops[e]:
                if op.is_dma:
                    continue
                if op.need_inc:
                    cnt += 1
                    op.inc_val = cnt
        for e in ENGS:
            eng = block_engines[e]
            waited = {}
            ops = self.ops[e]
            for op in ops:
                need = {}
                for d in op.deps:
                    if d.is_dma:
                        key = ("d", d.sem)
                        val = d.sem_val
                    else:
                        if d.eng == e and (e == "pe" or (op.pos - d.pos) > SAME_ENG_WIN):
                            continue
                        key = ("c", d.eng)
                        val = d.inc_val
                    if val > need.get(key, 0):
                        need[key] = val
                for key, val in need.items():
                    if waited.get(key, 0) >= val:
                        continue
                    waited[key] = val
                    sem = sems_dma[key[1]] if key[0] == "d" else sems_compute[key[1]]
                    eng.wait_ge(sem, val)
                ins = op.fn()
                if op.is_dma:
                    ins.then_inc(sems_dma[op.sem], 16)
                elif op.need_inc:
                    ins.then_inc(sems_compute[e], 1)
            if e == "sp":
                need = {}
                for d in self.final_dma:
                    need[d.sem] = max(need.get(d.sem, 0), d.sem_val)
                for si, val in need.items():
                    eng.wait_ge(sems_dma[si], val)


def _t5_bucket_np(dist):
    dist = np.asarray(dist, dtype=np.int32)
    d = np.maximum(dist, 1).astype(np.float32)
    large = 16 + (np.log(d / np.float32(16.0)) / np.float32(math.log(8.0)) * np.float32(16.0)).astype(np.int32)
    large = np.minimum(large, 31)
    return np.where(dist < 16, dist, large)


def make_consts():
    ident = np.eye(128, dtype=np.float32)
    tri = (np.arange(128)[:, None] <= np.arange(128)[None, :]).astype(np.float32)
    scanmask = np.ones((128, 256), np.float32)
    scanmask[:, 0] = 0.0
    scanmask[:, 128] = 0.0
    ohg = np.zeros((32, 384), np.float32)
    j = np.arange(127, 383)
    ohg[_t5_bucket_np(j - 127), j] = 1.0
    rev = np.ascontiguousarray(ident[::-1])
    return {"c_ident": ident, "c_tri": tri, "c_scanmask": scanmask, "c_ohg": ohg, "c_rev": rev}


def build(S=4096, NSEQ=2, mode="full"):
    nc = bass.Bass("TRN2", target_bir_lowering=False)
    TOK = NSEQ * S
    NST = S // 256
    sch = Sched(nc)

    def din(name, shape, dt=F32):
        return T(nc.dram_tensor(name, list(shape), dt, kind="ExternalInput").ap(), Buf(name))

    x_d = din("x", [TOK, D])
    c_d = din("c", [NSEQ, D])
    wada_d = din("w_ada", [D, 6 * D])
    bada_d = din("b_ada", [48, 128])
    win_d = din("w_in", [D, INW])
    lbl_d = din("lb_logits", [8, 128])
    hnw_d = din("hgrn_norm_w", [128, 1])
    lam_d = [din(n, [1, 64]) for n in ("lam_q1", "lam_k1", "lam_q2", "lam_k2")]
    dnw_d = din("diff_norm_w", [128, 1])
    relb_d = din("rel_bias", [32, 4])
    wout_d = din("w_out", [D, D])
    ln1g_d = din("ln1_g", [1, D])
    ln1b_d = din("ln1_b", [1, D])
    wg_d = din("w_gate", [D, DFF])
    wu_d = din("w_up", [D, DFF])
    wd_d = din("w_down", [DFF, D])
    ln2g_d = din("ln2_g", [1, D])
    ln2b_d = din("ln2_b", [1, D])
    cid_d = din("c_ident", [128, 128])
    ctri_d = din("c_tri", [128, 128])
    csm_d = din("c_scanmask", [128, 256])
    cohg_d = din("c_ohg", [32, 384])
    crev_d = din("c_rev", [128, 128])
    out_d = T(nc.dram_tensor("out", [TOK, D], F32, kind="ExternalOutput").ap(), Buf("out"))
    x1s_d = T(nc.dram_tensor("x1s", [TOK, D], F32, kind="Internal").ap(), Buf("x1s"))
    gscr_d = T(nc.dram_tensor("gscr", [4, 384], F32, kind="Internal").ap(), Buf("gscr"))

    import contextlib
    es = contextlib.ExitStack()
    with es:
        ARENA_W = 52992
        arena = es.enter_context(nc.sbuf_tensor("arena", [128, ARENA_W], F32))
        psum = [es.enter_context(nc.psum_tensor(f"ps{i}", [128, 512], F32)) for i in range(8)]
        PS = [T(psum[i][:, :], Buf(f"ps{i}")) for i in range(8)]

        class Alloc:
            def __init__(self, base, limit):
                self.off = base
                self.limit = limit

            def take(self, name, words, dt=F32, shape=None, buf=None):
                a = self.off
                self.off += (words + 7) // 8 * 8
                assert self.off <= self.limit, (name, self.off, self.limit)
                ap = arena[:, a:a + words]
                if dt == BF16:
                    ap = ap.bitcast(BF16)
                t = T(ap, buf if buf is not None else Buf(name))
                if shape is not None:
                    names = " ".join(f"d{i}" for i in range(len(shape)))
                    kw = {f"d{i}": s for i, s in enumerate(shape)}
                    t = t.rearrange(f"p ({names}) -> p {names}", **kw)
                return t

        AP_ = Alloc(0, ARENA_W)
        ident_f = AP_.take("ident_f", 128, shape=None)
        ident_b = AP_.take("ident_b", 64, BF16)
        ones_b = AP_.take("ones_b", 64, BF16)
        tri_b = AP_.take("tri_b", 64, BF16)
        zeros_f = AP_.take("zeros_f", 128)
        adaT = AP_.take("adaT", 48 * NSEQ, shape=[48, NSEQ])
        scl = AP_.take("scl", 16 * NSEQ, shape=[2, 8, NSEQ])
        smallv = AP_.take("smallv", 64)
        gbc = AP_.take("gbc", D)
        bbc = AP_.take("bbc", D)
        PERSIST_END = AP_.off

        lbT = smallv[:, 0:4]
        hnw = smallv[:, 8:9]
        dnw8 = smallv[:, 9:10]
        neglam = smallv[:, 10:11]
        crel = smallv[:, 11:12]
        lamt = smallv[:, 12:16]

        bank_rr = [0]

        def nextbank(pool):
            b = pool[bank_rr[0] % len(pool)]
            bank_rr[0] += 1
            return PS[b]

        def ln_stats(src, st6, mv, sm):
            sch.bn_stats(st6[:, 0:6], src[:, 0:512])
            sch.bn_stats(st6[:, 6:12], src[:, 512:1024])
            sch.bn_aggr(mv, st6)
            sch.act(sm[:, 2:3], mv[:, 1:2], AF.Ln, scale=1.0, bias=epsln)
            sch.act(sm[:, 0:1], sm[:, 2:3], AF.Exp, scale=-0.5)
            sch.stt(sm[:, 1:2], mv[:, 0:1], -1.0, sm[:, 0:1], ALU.mult, ALU.mult)

        def ln_mod_T(xts, xhat, U, sidx, which, banks, st6, mv, sm):
            pa, pb = banks
            pab = [pa.bitcast(BF16).rearrange("p (k t) -> p k t", k=4),
                   pb.bitcast(BF16).rearrange("p (k t) -> p k t", k=4)]
            for j in range(2):
                ln_stats(xts[j], st6[j], mv[j], sm[j])
                sch.act(xhat[:, j, :], xts[j], AF.Identity, scale=sm[j][:, 0:1], bias=sm[j][:, 1:2])
                for kc in range(8):
                    sch.tr(pab[kc // 4][:, kc % 4, j * 128:(j + 1) * 128], xhat[:, j, kc * 128:(kc + 1) * 128], ident_b)
            shift_base = 0 if which == 0 else 24
            for kc in range(8):
                src = pab[kc // 4][:, kc % 4, :]
                s_ap = scl[:, which, kc, sidx:sidx + 1]
                b_ap = adaT[:, shift_base + kc, sidx:sidx + 1]
                if kc % 2 == 0:
                    sch.act(U[:, kc, :], src, AF.Identity, scale=s_ap, bias=b_ap)
                else:
                    sch.ts("dve", U[:, kc, :], src, s_ap, ALU.mult, b_ap, ALU.add)

        def ln_final(t2, xn, ob, st6, mv, sm):
            ln_stats(t2, st6, mv, sm)
            sch.act(xn, t2, AF.Identity, scale=sm[:, 0:1], bias=sm[:, 1:2])
            sch.tt("pool", ob, xn, gbc, ALU.mult)
            sch.tt("pool", ob, ob, bbc, ALU.add)

        def make_bc_from_ada(dst, chunk0, sidx, plus_one, banks):
            for j in range(8):
                pb = nextbank(banks)
                rep = reptile
                sch.ts("dve", rep, zeros_f, adaT[:, chunk0 + j, sidx:sidx + 1], ALU.add,
                       1.0 if plus_one else 0.0, ALU.add)
                sch.tr(pb[:, 0:128], rep, ident_f)
                sch.copy("dve", dst[:, j * 128:(j + 1) * 128], pb[:, 0:128])

        P0 = Alloc(PERSIST_END, ARENA_W)
        epsln = P0.take("epsln", 8)
        sch.memset("dve", epsln[:, 0:1], LN_EPS)
        sch.memset("dve", epsln[:, 1:2], RMS_EPS)
        sch.memset("dve", epsln[:, 2:3], 1.0)
        epsrms = epsln[:, 1:2]
        onec = epsln[:, 2:3]
        epsln = epsln[:, 0:1]
        reptile = P0.take("reptile", 128)
        WIN_WORDS = 8 * INW // 2
        win = None
        if mode in ("full", "mixer"):
            win = P0.take("win", WIN_WORDS, BF16, shape=[8, INW])
            for kc in range(8):
                sch.dma("pool", win[:, kc, :], win_d[kc * 128:(kc + 1) * 128, :], key=Buf(f"winld{kc}"),
                        max_dma_last_dim=4096)
        stage = P0.take("stage", 384)
        sch.dma("sp", ident_f, cid_d, key=ident_f.bufs[0])
        sch.copy("dve", ident_b, ident_f)
        sch.memset("dve", ones_b, 1.0)
        sch.memset("dve", zeros_f, 0.0)
        sch.dma("sp", stage[:, 0:128], ctri_d, key=stage.bufs[0])
        sch.copy("dve", tri_b, stage[:, 0:128])
        sch.dma("sp", hnw, hnw_d, key=Buf("hnwld"))
        dnw_raw = P0.take("dnw_raw", 8)
        sch.dma("sp", dnw_raw[:, 0:1], dnw_d, key=dnw_raw.bufs[0])
        sch.ts("dve", dnw8, dnw_raw[:, 0:1], 1.0 - LAM_INIT, ALU.mult)
        lamv = P0.take("lamv", 256, shape=[4, 64])
        for i in range(4):
            sch.dma("sp", lamv[:, i, :], T(lam_d[i].ap.partition_broadcast(128), lam_d[i].bufs), key=Buf(f"lam{i}"))
        lprod = P0.take("lprod", 128, shape=[2, 64])
        sch.tt("dve", lprod[:, 0, :], lamv[:, 0, :], lamv[:, 1, :], ALU.mult)
        sch.tt("dve", lprod[:, 1, :], lamv[:, 2, :], lamv[:, 3, :], ALU.mult)
        nc_v = nc.vector
        sch.add("dve", lambda: nc_v.tensor_reduce(lamt[:, 0:2].ap, lprod.ap, mybir.AxisListType.X, ALU.add),
                [lprod], [lamt], name="red")
        sch.act(lamt[:, 2:4], lamt[:, 0:2], AF.Exp)
        sch.tt("dve", neglam, lamt[:, 3:4], lamt[:, 2:3], ALU.subtract)
        sch.ts("dve", neglam, neglam, -LAM_INIT, ALU.add)
        lbrow = P0.take("lbrow", 128)
        sch.dma("sp", lbrow[0:8, :], lbl_d, key=lbrow.bufs[0])
        pb = nextbank([4, 5, 6, 7])
        sch.tr(pb[:, 0:8], lbrow[0:8, :], ident_f[0:8, 0:8])
        lbtmp = P0.take("lbtmp", 8)
        sch.copy("dve", lbtmp[:, 0:8], pb[:, 0:8])
        sch.tt("dve", lbtmp[:, 0:4], lbtmp[:, 0:4], lbtmp[:, 4:8], ALU.subtract)
        sch.act(lbtmp[:, 4:8], lbtmp[:, 0:4], AF.Exp, scale=-1.0)
        sch.ts("dve", lbtmp[:, 4:8], lbtmp[:, 4:8], 1.0, ALU.add)
        sch.recip(lbT, lbtmp[:, 4:8])
        crow = P0.take("crow", D)
        sch.dma("sp", crow[0:NSEQ, :], c_d, key=crow.bufs[0])
        cwork = P0.take("cwork", D)
        sch.act(cwork[0:NSEQ, :], crow[0:NSEQ, :], AF.Exp, scale=-1.0)
        sch.ts("dve", cwork[0:NSEQ, :], cwork[0:NSEQ, :], 1.0, ALU.add)
        sch.recip(cwork[0:NSEQ, :], cwork[0:NSEQ, :])
        sch.tt("dve", crow[0:NSEQ, :], crow[0:NSEQ, :], cwork[0:NSEQ, :], ALU.mult)
        scT = P0.take("scT", 8 * NSEQ, BF16, shape=[8, 2 * NSEQ])[:, :, 0:NSEQ]
        pb = nextbank([4, 5, 6, 7])
        for kc in range(8):
            sch.tr(pb[:, kc * NSEQ:(kc + 1) * NSEQ], crow[0:NSEQ, kc * 128:(kc + 1) * 128], ident_f[0:NSEQ, 0:NSEQ])
        sch.copy("dve", scT, pb[:, 0:8 * NSEQ].rearrange("p (k s) -> p k s", s=NSEQ))
        badaT = P0.take("badaT", 48)
        brow = P0.take("brow", 128)
        sch.dma("sp", brow[0:48, :], bada_d, key=brow.bufs[0])
        pb = nextbank([4, 5, 6, 7])
        sch.tr(pb[:, 0:48], brow[0:48, :], ident_f[0:48, 0:48])
        sch.copy("dve", badaT, pb[:, 0:48])
        wst = [P0.take(f"wada{i}", 8 * D // 2, BF16, shape=[8, D]) for i in range(3)]
        for piece in range(6):
            w = wst[piece % 3]
            sch.dma("pool", w, T(wada_d.ap[:, piece * D:(piece + 1) * D].rearrange("(k p) n -> p k n", p=128), wada_d.bufs),
                    key=w.bufs[0], max_dma_last_dim=4096)
            pb = nextbank([4, 5, 6, 7])
            for j in range(8):
                for kc in range(8):
                    sch.mm(pb[:, j * NSEQ:(j + 1) * NSEQ], w[:, kc, j * 128:(j + 1) * 128], scT[:, kc, :],
                           start=(kc == 0), stop=(kc == 7))
            for s_ in range(NSEQ):
                sch.tt("dve", adaT[:, piece * 8:(piece + 1) * 8, s_],
                       pb[:, 0:8 * NSEQ].rearrange("p (j s) -> p j s", s=NSEQ)[:, :, s_],
                       badaT[:, piece * 8:(piece + 1) * 8], ALU.add)
        sch.ts("dve", scl[:, 0, :, :], adaT[:, 8:16, :], 1.0, ALU.add)
        sch.ts("dve", scl[:, 1, :, :], adaT[:, 32:40, :], 1.0, ALU.add)

        if mode in ("full", "mixer"):
            sch.barrier()
            A1 = Alloc(PERSIST_END, ARENA_W)
            epsln_k = A1.take("eps_keep", 8)
            reptile = A1.take("reptile1", 128)
            A1.take("win_keep", WIN_WORDS)
            wout = A1.take("wout", 8 * D // 2, BF16, shape=[8, D])
            kT = A1.take("kT", 4 * S // 2, BF16, shape=[4, S])
            Vc = A1.take("Vc", (S // 128) * 512 // 2, BF16, shape=[S // 128, 512])
            xt_buf = [Buf("xt0"), Buf("xt1")]
            xtile = A1.take("xt", 2 * D, shape=[2, D])
            xts = [xtile[:, j, :].on(xt_buf[j]) for j in range(2)]
            U = A1.take("U", 8 * 256 // 2, BF16, shape=[8, 256])
            qpad = A1.take("qpad", 4 * 2 * 256 // 2, BF16, shape=[4, 2, 256])
            G = [A1.take(f"G{i}", D) for i in range(5)]
            qhat = A1.take("qhat", 512, BF16, shape=[4, 256])
            khat = A1.take("khat", 512, BF16, shape=[4, 256])
            gate = A1.take("gate", 512, BF16, shape=[4, 256])
            vh = A1.take("vh", 512, BF16, shape=[2, 512])
            Am = A1.take("Am", 256, BF16, shape=[4, 128])
            khT = A1.take("khT", 256, BF16, shape=[4, 128])
            Sbf = A1.take("Sbf", 256, BF16, shape=[4, 128])
            Sst = A1.take("Sst", 512, shape=[4, 128])
            PT = [A1.take(f"PT{i}", 256, BF16, shape=[2, 256]) for i in range(2)]
            biasT = A1.take("biasT", 512, BF16, shape=[4, 256])
            scanmask = A1.take("scanmask", 256)
            smt = A1.take("smt", 64)
            lnsm = [A1.take(f"lnsm{i}", 24) for i in range(4)]
            rlb = A1.take("rlb", 512, shape=[2, 256])
            odb = A1.take("odb", 256)
            sqb = A1.take("sqb", 128, BF16)
            rsb = A1.take("rsb", 256)
            print("phase1 arena words used", A1.off, "of", ARENA_W)

            BK = [6, 2, 3, 7, 0, 1, 4, 5]
            BKB = [6]
            BKP = [7]

            sch.dma("sp", scanmask, csm_d, key=scanmask.bufs[0])
            relb = G[0][0:32, 0:4]
            sch.dma("sp", relb, relb_d, key=Buf("relbld"))
            ohg_c = G[1][0:32, 0:384]
            sch.dma("sp", ohg_c, cohg_d, key=Buf("ohgld"))
            sch.dma("sp", crel[0:4, :], T(relb_d.ap[31:32, :].rearrange("o h -> h o"), relb_d.bufs), key=Buf("crelld"))
            pb = nextbank(BK)
            sch.mm(pb[0:4, 0:384], relb, ohg_c)
            grow = G[2][0:4, 0:384]
            sch.ts("dve", grow, pb[0:4, 0:384], crel[0:4, :], ALU.subtract, 8.0, ALU.mult)
            sch.memset("dve", grow[:, 0:127], MASKVAL)
            sch.dma("sp", gscr_d, grow, key=Buf("gscrst"))
            btf = G[3].rearrange("p (h j) -> p h j", h=4)
            rev_f = G[0][:, 512:640]
            sch.dma("sp", rev_f, crev_d, key=Buf("revld"))
            for h in range(4):
                src = bass.AP(tensor=gscr_d.ap.tensor, offset=h * 384, ap=[[1, 128], [1, 256]])
                sch.dma("sp", btf[:, h, :], T(src, gscr_d.bufs), key=Buf(f"btf{h}"))
                pb = nextbank(BK)
                sch.mm(pb[:, 0:256], rev_f, btf[:, h, :])
                sch.copy("dve", biasT[:, h, :], pb[:, 0:256])
            sch.memset("pool", qpad, 0.0)

            def load_wout(sidx):
                make_bc_from_ada(G[4], 16, sidx, True, BK)
                for kc in range(8):
                    stg = G[kc % 4]
                    sch.dma("sp", stg, wout_d[kc * 128:(kc + 1) * 128, :], key=stg.bufs[0])
                    sch.tt("dve", wout[:, kc, :], stg, G[4], ALU.mult)

            sch.dma("sp", gbc, T(ln1g_d.ap.partition_broadcast(128), ln1g_d.bufs), key=gbc.bufs[0])
            sch.dma("sp", bbc, T(ln1b_d.ap.partition_broadcast(128), ln1b_d.bufs), key=bbc.bufs[0])

            st6 = [lnsm[j][:, 0:12] for j in range(4)]
            mv = [lnsm[j][:, 12:14] for j in range(4)]
            sm = [lnsm[j][:, 16:20] for j in range(4)]
            items = [(sidx, st) for sidx in range(NSEQ) for st in range(NST)]
            NI = len(items)
            xhat = G[4].bitcast(BF16).rearrange("p (j d) -> p j d", j=2)
            uT = G[4].bitcast(BF16).rearrange("p (k t) -> p k t", k=8)
            E, L1, L2, hqs = G[0], G[1], G[2], G[3]
            E4 = E.rearrange("p (h t) -> p h t", h=4)
            L14 = L1.rearrange("p (h t) -> p h t", h=4)
            L24 = L2.rearrange("p (h t) -> p h t", h=4)
            hq4 = hqs.rearrange("p (h t) -> p h t", h=4)

            def A_pre(i):
                sidx, st = items[i]
                tok0 = sidx * S + st * 256
                for j in range(2):
                    sch.dma("sp", xts[j], x_d[tok0 + j * 128: tok0 + (j + 1) * 128, :], key=xt_buf[j])
                    yield
                for j in range(2):
                    ln_stats(xts[j], st6[j], mv[j], sm[j])
                    yield
                    sch.act(xhat[:, j, :], xts[j], AF.Identity, scale=sm[j][:, 0:1], bias=sm[j][:, 1:2])
                    yield

            def A_tr(i):
                sidx, st = items[i]
                pa, pb = PS[0], PS[1]
                bank_rr[0] = 0
                pab = [pa.bitcast(BF16).rearrange("p (k t) -> p k t", k=4),
                       pb.bitcast(BF16).rearrange("p (k t) -> p k t", k=4)]
                for j in range(2):
                    for kc in range(8):
                        sch.tr(pab[kc // 4][:, kc % 4, j * 128:(j + 1) * 128], xhat[:, j, kc * 128:(kc + 1) * 128], ident_b)
                    yield
                for kc in range(8):
                    src = pab[kc // 4][:, kc % 4, :]
                    s_ap = scl[:, 0, kc, sidx:sidx + 1]
                    b_ap = adaT[:, kc, sidx:sidx + 1]
                    if kc % 2 == 0:
                        sch.act(uT[:, kc, :], src, AF.Identity, scale=s_ap, bias=b_ap)
                    else:
                        sch.ts("dve", uT[:, kc, :], src, s_ap, ALU.mult, b_ap, ALU.add)
                    if kc % 2 == 1:
                        yield

            def A_pre_tr(i):
                yield from A_pre(i)
                yield from A_tr(i)

            def A_proj(i):
                sidx, st = items[i]
                q0 = st * 256

                def fm_group(col0, h0):
                    pb = nextbank(BK)
                    pv = pb.rearrange("p (h t) -> p h t", h=2)
                    for hh in range(2):
                        c0 = col0 + (h0 + hh) * 128
                        for kc in range(8):
                            sch.mm(pv[:, hh, :], win[:, kc, c0:c0 + 128], uT[:, kc, :], start=(kc == 0), stop=(kc == 7))
                    return pv

                for h0 in (0, 2):
                    pv = fm_group(1536, h0)
                    tmpg = L24[:, h0:h0 + 2, :]
                    sch.act(tmpg, pv, AF.Exp, scale=-1.0)
                    sch.act(tmpg, tmpg, AF.Ln, scale=1.0, bias=onec)
                    sch.act(tmpg, tmpg, AF.Exp, scale=-1.0)
                    sch.tt("dve", gate[:, h0:h0 + 2, :], pv, tmpg, ALU.mult)
                for h0 in (0, 2):
                    pv = fm_group(512, h0)
                    sch.act(E4[:, h0:h0 + 2, :], pv, AF.Exp, scale=-1.0)
                for h0 in (0, 2):
                    pv = fm_group(0, h0)
                    sch.copy("dve", hq4[:, h0:h0 + 2, :], pv)
                for j in range(2):
                    for (col0, dst) in ((1024, vh[:, j, :]), (3072, Vc[:, st * 2 + j, :])):
                        pb = nextbank(BK)
                        for kc in range(8):
                            sch.mm(pb, uT[:, kc, j * 128:(j + 1) * 128], win[:, kc, col0:col0 + 512],
                                   start=(kc == 0), stop=(kc == 7))
                        sch.copy("act" if col0 == 1024 else "dve", dst, pb)
                for h0 in (0, 2):
                    pv = fm_group(2048, h0)
                    sch.copy("act", qpad[0:64, h0:h0 + 2, 0, :], pv[0:64, :, :])
                    sch.copy("dve", qpad[64:128, h0:h0 + 2, 1, :], pv[64:128, :, :])
                for h0 in (0, 2):
                    pv = fm_group(2560, h0)
                    sch.copy("act", kT[:, h0:h0 + 2, q0:q0 + 256], pv)

            sm8 = smt[:, 0:16].rearrange("p (g two) -> p g two", two=2)
            ebm = smt[:, 16:24]
            d1 = smt[:, 24:32]
            d2 = smt[:, 32:40]

            def B_pre(i):
                sch.act(L2, E, AF.Ln, scale=1.0, bias=onec)
                yield
                for h in range(4):
                    sch.act(L14[:, h, :], E4[:, h, :], AF.Ln, scale=lbT[:, h:h + 1], bias=onec)
                yield
                sch.tt("dve", L1, L1, L2, ALU.subtract)
                yield
                sch.act(E, L1, AF.Exp)
                yield
                sch.act(E, E, AF.Identity, scale=-1.0, bias=onec)
                for h in range(4):
                    sch.scan(L24[:, h, :], scanmask, L14[:, h, :], 0.0, ALU.mult, ALU.add)
                    yield
                b8 = L2.rearrange("p (g t) -> p g t", t=128)
                sch.copy("dve", sm8, b8[:, :, 63:128:64])
                yield
                sch.tt("dve", b8, b8, sm8[:, :, 0:1].bc([128, 8, 128]), ALU.subtract)
                sch.tt("dve", d2, sm8[:, :, 1], sm8[:, :, 0], ALU.subtract)
                yield
                sch.act(L1, L2, AF.Exp)
                yield
                sch.act(L2, L2, AF.Exp, scale=-1.0)
                sch.act(ebm, sm8[:, :, 0], AF.Exp)
                sch.act(d1, sm8[:, :, 1], AF.Exp)
                sch.act(d2, d2, AF.Exp)
                yield
                sch.stt(qhat, hq4, 128.0 ** -0.5, L14, ALU.mult, ALU.mult)
                yield
                sch.tt("dve", khat[:, 0:2, :], E4[:, 0:2, :], L24[:, 0:2, :], ALU.mult)
                sch.tt("pool", khat[:, 2:4, :], E4[:, 2:4, :], L24[:, 2:4, :], ALU.mult)
                yield

            def B_gen(i):
                ohg = G[0].rearrange("p (h t) -> p h t", h=4)
                for cch in range(2):
                    tsl = slice(cch * 128, (cch + 1) * 128)
                    pA = nextbank(BKB)
                    pA4 = pA.rearrange("p (h t) -> p h t", h=4)
                    for h in range(4):
                        sch.mm(pA4[:, h, :], khat[:, h, tsl], qhat[:, h, tsl])
                    yield
                    sch.tt("dve", Am, pA4, tri_b.rearrange("p (o t) -> p o t", o=1).bc([128, 4, 128]), ALU.mult)
                    pK = nextbank(BKB)
                    pK4 = pK[:, 0:256].bitcast(BF16).rearrange("p (h t) -> p h t", h=4)
                    for h in range(4):
                        sch.tr(pK4[:, h, :], khat[:, h, tsl], ident_b)
                    yield
                    sch.copy("dve", khT, pK4)
                    g8 = lambda v: v.rearrange("p (h c) -> p h c", c=2)[:, :, cch:cch + 1].bc([128, 4, 128])
                    sch.tt("dve", Sbf, Sst, g8(ebm), ALU.mult)
                    yield
                    pO = nextbank(BKB)
                    pO4 = pO.rearrange("p (h t) -> p h t", h=4)
                    for h in range(4):
                        sch.mm(pO4[:, h, :], vh[:, cch, h * 128:(h + 1) * 128], Am[:, h, :], start=True, stop=False)
                        sch.mm(pO4[:, h, :], Sbf[:, h, :], qhat[:, h, tsl], start=False, stop=True)
                    yield
                    sch.copy("dve", ohg[:, :, tsl], pO4)
                    pP = nextbank(BKB)
                    pP4 = pP.rearrange("p (h t) -> p h t", h=4)
                    for h in range(4):
                        sch.mm(pP4[:, h, :], khT[:, h, :], vh[:, cch, h * 128:(h + 1) * 128])
                    yield
                    stmp = G[1][:, 0:512].rearrange("p (h t) -> p h t", h=4)
                    stmp2 = G[1][:, 512:1024].rearrange("p (h t) -> p h t", h=4)
                    sch.tt("pool", stmp, Sst, g8(d1), ALU.mult)
                    sch.tt("dve", stmp2, pP4, g8(d2), ALU.mult)
                    yield
                    sch.tt("dve", Sst, stmp, stmp2, ALU.add)
                    yield
                srcf = G[0]
                sq = G[2].bitcast(BF16)[:, 0:1024]
                sch.act(sq, srcf, AF.Square)
                yield
                rs = G[3]
                for half in range(2):
                    pS = nextbank(BKB)
                    sch.mm(pS, ones_b, sq[:, half * 512:(half + 1) * 512])
                    yield
                    sch.act(rs[:, half * 512:(half + 1) * 512], pS, AF.Ln, scale=1.0 / 128.0, bias=epsrms)
                    yield
                sch.act(rs, rs, AF.Exp, scale=-0.5)
                yield
                sch.tt("dve", rs, rs, srcf, ALU.mult)
                yield
                rs4 = rs.rearrange("p (h t) -> p h t", h=4)
                sch.stt(U[:, 0:4, :], rs4, hnw, gate, ALU.mult, ALU.mult)
                yield

            def C_attn(i, gens):
                sidx, st = items[i]
                q0 = st * 256
                qt0 = q0 // 128
                nkt = qt0 + 2
                live = list(gens)
                post_gen = [None]

                def bg():
                    for g in list(live):
                        try:
                            next(g)
                        except StopIteration:
                            live.remove(g)

                for h in range(4):
                    pO = PS[2 + 2 * (h % 2)]
                    pL = PS[3 + 2 * (h % 2)]
                    pOv = pO.rearrange("p (m t) -> p m t", m=2)
                    pLv = pL.rearrange("p (m t) -> p m t", m=2)

                    def qk(kt):
                        pS = PS[kt % 2]
                        pSv = pS.rearrange("p (m t) -> p m t", m=2)
                        c0 = 128 if kt == qt0 + 1 else 0
                        if kt == qt0 - 1:
                            win_q, win_b = (0, 128), (128, 256)
                        elif kt == qt0:
                            win_q, win_b = (0, 256), (0, 256)
                        elif kt == qt0 + 1:
                            win_q, win_b = (128, 256), (0, 128)
                        else:
                            win_q = None
                        for m in range(2):
                            sch.mm(pSv[:, m, c0:256], kT[:, h, kt * 128:(kt + 1) * 128], qpad[:, h, m, c0:256],
                                   start=True, stop=(win_q is None))
                            if win_q is not None:
                                sch.mm(pSv[:, m, win_q[0]:win_q[1]], ident_b, biasT[:, h, win_b[0]:win_b[1]],
                                       start=False, stop=True)
                        pt = PT[kt % 2]
                        sch.act(pt[:, :, c0:256], pSv[:, :, c0:256], AF.Exp, scale=0.125)
                        return c0

                    def pv(kt, c0):
                        pt = PT[kt % 2]
                        sch.mm(pOv[:, :, c0:256], Vc[:, kt, h * 128:(h + 1) * 128], pt[:, :, c0:256],
                               start=(kt == 0), stop=(kt == nkt - 1))
                        sch.mm(pLv[:, :, c0:256], ones_b, pt[:, :, c0:256],
                               start=(kt == 0), stop=(kt == nkt - 1))

                    c0s = {0: qk(0)}
                    for kt in range(nkt):
                        if kt + 1 < nkt:
                            c0s[kt + 1] = qk(kt + 1)
                        pv(kt, c0s[kt])
                        bg()
                    if post_gen[0] is not None:
                        for _ in post_gen[0]:
                            pass
                        if post_gen[0] in live:
                            live.remove(post_gen[0])

                    def post(h=h, pOv=pOv, pLv=pLv):
                        sch.act(rlb, pLv, AF.Ln)
                        yield
                        sch.act(rlb, rlb, AF.Exp, scale=-1.0)
                        yield
                        sch.tt("dve", rlb, pOv, rlb, ALU.mult)
                        yield
                        sch.stt(odb, rlb[:, 1, :], neglam, rlb[:, 0, :], ALU.mult, ALU.add)
                        yield
                        sch.tt("pool", sqb, odb, odb, ALU.mult)
                        yield
                        yield
                        pS = nextbank(BKP)
                        sch.mm(pS[:, 0:256], ones_b, sqb)
                        yield
                        sch.act(rsb, pS[:, 0:256], AF.Ln, scale=1.0 / 128.0, bias=epsrms)
                        yield
                        sch.act(rsb, rsb, AF.Exp, scale=-0.5)
                        yield
                        sch.tt("dve", rsb, rsb, odb, ALU.mult)
                        yield
                        sch.ts("dve", U[:, 4 + h, :], rsb, dnw8, ALU.mult)

                    post_gen[0] = post()
                    live.insert(0, post_gen[0])
                while live:
                    bg()

            def D_ld(i, j):
                sidx, st = items[i]
                tok0 = sidx * S + st * 256
                t2s = xts
                sch.dma("sp", t2s[j], x_d[tok0 + j * 128: tok0 + (j + 1) * 128, :], key=xt_buf[j])

            def D_mm(i):
                sidx, st = items[i]
                tok0 = sidx * S + st * 256
                t2s = xts
                for j in range(2):
                    pM = [nextbank(BK), nextbank(BK)]
                    for n in range(2):
                        for kc in range(8):
                            sch.mm(pM[n], U[:, kc, j * 128:(j + 1) * 128], wout[:, kc, n * 512:(n + 1) * 512],
                                   start=(kc == 0), stop=(kc == 7))
                    t2 = t2s[j]
                    for n in range(2):
                        sch.stt(t2[:, n * 512:(n + 1) * 512], t2[:, n * 512:(n + 1) * 512], ALPHA, pM[n],
                                ALU.mult, ALU.add)

            def D_fin(i):
                sidx, st = items[i]
                tok0 = sidx * S + st * 256
                t2s = xts
                for j in range(2):
                    t2 = t2s[j]
                    ln_final(t2, t2, t2, st6[2 + j], mv[2 + j], sm[2 + j])
                    dst = x1s_d if mode == "full" else out_d
                    sch.dma("pool", dst[tok0 + j * 128: tok0 + (j + 1) * 128, :], t2, key=xt_buf[j],
                            final=(mode != "full"))

            def drain(g):
                for _ in g:
                    pass

            for i in range(NI):
                sidx, st = items[i]
                if st == 0:
                    load_wout(sidx)
                    sch.memset("pool", Sst, 0.0)
                    drain(A_pre_tr(i))
                    A_proj(i)
                    drain(B_pre(i))
                nxt_same = (i + 1 < NI) and items[i + 1][1] != 0
                def Bfull(i=i):
                    if st != 0:
                        yield from B_pre(i)
                    yield from B_gen(i)
                gens = [Bfull()]
                if nxt_same:
                    gens.append(A_pre(i + 1))
                C_attn(i, gens)
                D_ld(i, 0)
                D_ld(i, 1)
                if nxt_same:
                    drain(A_tr(i + 1))
                    A_proj(i + 1)
                D_mm(i)
                D_fin(i)

        if mode in ("full", "ffn"):
            sch.barrier()
            A2 = Alloc(PERSIST_END, ARENA_W)
            epsln_k = A2.take("eps_keep2", 8)
            reptile = A2.take("reptile2", 128)
            wg = A2.take("wg", 8 * DFF // 2, BF16, shape=[8, DFF])
            wu = A2.take("wu", 8 * DFF // 2, BF16, shape=[8, DFF])
            wd = A2.take("wd", NF * D // 2, BF16, shape=[NF, D])
            x1t = [[A2.take(f"x1t{s}{j}", D) for j in range(2)] for s in range(2)]
            xhat2 = A2.take("xhat2", D, BF16, shape=[2, D])
            U2 = [A2.take(f"U2{i}", 8 * 256 // 2, BF16, shape=[8, 256]) for i in range(2)]
            hT = A2.take("hT", NF * 256 // 2, BF16, shape=[NF, 256])
            hTf = [hT[:, f, :].on(Buf(f"hT{f}")) for f in range(NF)]
            sw = [[A2.take(f"sw{i}{k}", 256) for k in range(2)] for i in range(2)]
            tb = A2.take("tb", D)
            obb = [A2.take(f"obb{i}", D) for i in range(2)]
            gmbc = [A2.take(f"gmbc{i}", D) for i in range(NSEQ)]
            lnsm2 = [A2.take(f"lnsm2{i}", 24) for i in range(4)]
            print("phase2 arena words used", A2.off, "of", ARENA_W)
            BK2 = [0, 1, 2, 3, 4, 5, 6, 7]
            for kc in range(8):
                sch.dma("pool", wg[:, kc, :], wg_d[kc * 128:(kc + 1) * 128, :], key=Buf(f"wgld{kc}"), max_dma_last_dim=4096)
                sch.dma("pool", wu[:, kc, :], wu_d[kc * 128:(kc + 1) * 128, :], key=Buf(f"wuld{kc}"), max_dma_last_dim=4096)
            for f in range(NF):
                sch.dma("pool", wd[:, f, :], wd_d[f * 128:(f + 1) * 128, :], key=Buf(f"wdld{f}"), max_dma_last_dim=4096)
            sch.dma("sp", gbc, T(ln2g_d.ap.partition_broadcast(128), ln2g_d.bufs), key=gbc.bufs[0])
            sch.dma("sp", bbc, T(ln2b_d.ap.partition_broadcast(128), ln2b_d.bufs), key=bbc.bufs[0])
            for sidx in range(NSEQ):
                make_bc_from_ada(gmbc[sidx], 40, sidx, True, BK2)
            src_d = x1s_d if mode == "full" else x_d
            st6 = [lnsm2[j][:, 0:12] for j in range(4)]
            mv = [lnsm2[j][:, 12:14] for j in range(4)]
            sm = [lnsm2[j][:, 16:20] for j in range(4)]
            items = [(sidx, st) for sidx in range(NSEQ) for st in range(NST)]

            def p2_lnA(i):
                sidx, st = items[i]
                tok0 = sidx * S + st * 256
                xts2 = x1t[i % 2]
                for j in range(2):
                    sch.dma("sp", xts2[j], src_d[tok0 + j * 128: tok0 + (j + 1) * 128, :], key=xts2[j].bufs[0])
                for j in range(2):
                    ln_stats(xts2[j], st6[j], mv[j], sm[j])
                    sch.act(xhat2[:, j, :], xts2[j], AF.Identity, scale=sm[j][:, 0:1], bias=sm[j][:, 1:2])

            def p2_trB(i):
                sidx, st = items[i]
                pa, pb = nextbank(BK2), nextbank(BK2)
                pab = [pa.bitcast(BF16).rearrange("p (k t) -> p k t", k=4),
                       pb.bitcast(BF16).rearrange("p (k t) -> p k t", k=4)]
                for j in range(2):
                    for kc in range(8):
                        sch.tr(pab[kc // 4][:, kc % 4, j * 128:(j + 1) * 128], xhat2[:, j, kc * 128:(kc + 1) * 128], ident_b)
                U = U2[i % 2]
                for kc in range(8):
                    src = pab[kc // 4][:, kc % 4, :]
                    s_ap = scl[:, 1, kc, sidx:sidx + 1]
                    b_ap = adaT[:, 24 + kc, sidx:sidx + 1]
                    if kc % 2 == 0:
                        sch.act(U[:, kc, :], src, AF.Identity, scale=s_ap, bias=b_ap)
                    else:
                        sch.ts("dve", U[:, kc, :], src, s_ap, ALU.mult, b_ap, ALU.add)

            def p2_gu(i):
                U = U2[i % 2]
                for f in range(NF):
                    pb = nextbank(BK2)
                    pv = pb.rearrange("p (g t) -> p g t", g=2)
                    for gi, w in enumerate((wg, wu)):
                        for kc in range(8):
                            sch.mm(pv[:, gi, :], w[:, kc, f * 128:(f + 1) * 128], U[:, kc, :],
                                   start=(kc == 0), stop=(kc == 7))
                    ee, rr = sw[f % 2]
                    sch.act(ee, pv[:, 0, :], AF.Exp, scale=-1.0)
                    sch.act(ee, ee, AF.Ln, scale=1.0, bias=onec)
                    sch.act(rr, ee, AF.Exp, scale=-1.0)
                    sch.tt("dve", rr, pv[:, 0, :], rr, ALU.mult)
                    sch.tt("dve", hTf[f], rr, pv[:, 1, :], ALU.mult)

            def p2_downmm(i, j):
                pY = [nextbank(BK2), nextbank(BK2)]
                for n in range(2):
                    for f in range(NF):
                        sch.mm(pY[n], hTf[f][:, j * 128:(j + 1) * 128], wd[:, f, n * 512:(n + 1) * 512],
                               start=(f == 0), stop=(f == NF - 1))
                return pY

            def p2_fin(i, j, pY):
                sidx, st = items[i]
                tok0 = sidx * S + st * 256
                xts2 = x1t[i % 2]
                for n in range(2):
                    sch.tt("dve", tb[:, n * 512:(n + 1) * 512], pY[n], gmbc[sidx][:, n * 512:(n + 1) * 512], ALU.mult)
                sch.stt(tb, xts2[j], ALPHA, tb, ALU.mult, ALU.add)
                ob = obb[j]
                ln_final(tb, ob, ob, st6[2 + j], mv[2 + j], sm[2 + j])
                sch.dma("pool", out_d[tok0 + j * 128: tok0 + (j + 1) * 128, :], ob, key=ob.bufs[0], final=True)

            NI = len(items)
            p2_lnA(0)
            p2_trB(0)
            for i in range(NI):
                p2_gu(i)
                if i + 1 < NI:
                    p2_lnA(i + 1)
                pY0 = p2_downmm(i, 0)
                if i + 1 < NI:
                    p2_trB(i + 1)
                pY1 = p2_downmm(i, 1)
                p2_fin(i, 0, pY0)
                p2_fin(i, 1, pY1)

        nsem_dma = len(sch.dma_sem_cnt)
        print("ops:", {e: len(sch.ops[e]) for e in ENGS}, "dma sems:", nsem_dma)
        sems_c = {e: es.enter_context(nc.semaphore(f"sc_{e}")) for e in ENGS}
        sems_d = [es.enter_context(nc.semaphore(f"sd_{i}")) for i in range(nsem_dma)]
        block = es.enter_context(nc.Block())
        engs = {}

        @block.tensor
        def _(e):
            engs["pe"] = e
            sch.emit_one = None
            _emit_engine(sch, "pe", e, sems_c, sems_d)

        @block.scalar
        def _(e):
            _emit_engine(sch, "act", e, sems_c, sems_d)

        @block.vector
        def _(e):
            _emit_engine(sch, "dve", e, sems_c, sems_d)

        @block.gpsimd
        def _(e):
            _emit_engine(sch, "pool", e, sems_c, sems_d)

        @block.sync
        def _(e):
            _emit_engine(sch, "sp", e, sems_c, sems_d)
    return nc


_prepared = set()


def _prepare(sch):
    if id(sch) in _prepared:
        return
    _prepared.add(id(sch))
    for e in ENGS:
        for op in sch.ops[e]:
            for d in op.deps:
                if d.is_dma:
                    continue
                if d.eng != op.eng:
                    d.need_inc = True
                elif op.eng != "pe" and (op.pos - d.pos) <= SAME_ENG_WIN:
                    d.need_inc = True
    for e in ENGS:
        cnt = 0
        for op in sch.ops[e]:
            if op.is_dma:
                continue
            if op.need_inc:
                cnt += 1
                op.inc_val = cnt


def _emit_engine(sch, e, eng, sems_compute, sems_dma):
    _prepare(sch)
    waited = {}
    for op in sch.ops[e]:
        need = {}
        for d in op.deps:
            if d.is_dma:
                key = ("d", d.sem)
                val = d.sem_val
            else:
                if d.eng == e and (e == "pe" or (op.pos - d.pos) > SAME_ENG_WIN):
                    continue
                key = ("c", d.eng)
                val = d.inc_val
            if val > need.get(key, 0):
                need[key] = val
        for key, val in need.items():
            if waited.get(key, 0) >= val:
                continue
            waited[key] = val
            sem = sems_dma[key[1]] if key[0] == "d" else sems_compute[key[1]]
            eng.wait_ge(sem, val)
        ins = op.fn()
        if op.is_dma:
            ins.then_inc(sems_dma[op.sem], 16)
        elif op.need_inc:
            ins.then_inc(sems_compute[e], 1)
    if e == "sp":
        need = {}
        for d in sch.final_dma:
            need[d.sem] = max(need.get(d.sem, 0), d.sem_val)
        for si, val in need.items():
            eng.wait_ge(sems_dma[si], val)


_NC_CACHE = {}


def _core_inputs(inp, sl, S, nseq):
    f = lambda a: np.ascontiguousarray(np.asarray(a, dtype=np.float32))
    m = {
        "x": f(inp["x"][sl]).reshape(nseq * S, D),
        "c": f(inp["c"][sl]),
        "w_ada": f(inp["w_ada"][0]),
        "b_ada": f(inp["b_ada"][0]).reshape(48, 128),
        "w_in": f(inp["w_in"][0]),
        "lb_logits": f(inp["lb_logits"]).reshape(8, 128),
        "hgrn_norm_w": f(inp["hgrn_norm_w"][0]).reshape(128, 1),
        "lam_q1": f(inp["lam_q1"]), "lam_k1": f(inp["lam_k1"]),
        "lam_q2": f(inp["lam_q2"]), "lam_k2": f(inp["lam_k2"]),
        "diff_norm_w": f(inp["diff_norm_w"][0]).reshape(128, 1),
        "rel_bias": f(inp["rel_bias"]),
        "w_out": f(inp["w_out"][0]),
        "ln1_g": f(inp["ln1_g"]), "ln1_b": f(inp["ln1_b"]),
        "w_gate": f(inp["w_gate"][0]), "w_up": f(inp["w_up"][0]), "w_down": f(inp["w_down"][0]),
        "ln2_g": f(inp["ln2_g"]), "ln2_b": f(inp["ln2_b"]),
    }
    m.update(make_consts())
    return m


def kernel(**inputs):
    x = np.asarray(inputs["x"])
    B, S, _ = x.shape
    ncores = 8
    nseq = B // ncores
    key = (S, nseq)
    if key not in _NC_CACHE:
        _NC_CACHE[key] = build(S=S, NSEQ=nseq, mode="full")
    nc = _NC_CACHE[key]
    in_maps = [_core_inputs(inputs, slice(i * nseq, (i + 1) * nseq), S, nseq) for i in range(ncores)]
    res = run_bass_kernel_spmd(nc, in_maps, core_ids=list(range(ncores)))
    outs = [np.asarray(r["out"]).reshape(nseq, S, D) for r in res.results]
    return np.concatenate(outs, axis=0).astype(np.float32)
```

```python
import math
import numpy as np
import concourse.bass as bass
import concourse.mybir as mybir
from concourse.bass_utils import run_bass_kernel_spmd

F32 = mybir.dt.float32
BF16 = mybir.dt.bfloat16
AF = mybir.ActivationFunctionType
ALU = mybir.AluOpType

D = 1024
DFF = 2816
NF = DFF // 128
INW = 3584
LN_EPS = 1e-5
RMS_EPS = 1e-6
ALPHA = 2.0 ** 0.25
LAM_INIT = 0.8 - 0.6 * math.exp(0.0)
MASKVAL = -240000.0


class Buf:
    __slots__ = ("name", "last_w", "readers")

    def __init__(self, name):
        self.name = name
        self.last_w = None
        self.readers = []


class T:
    __slots__ = ("ap", "bufs")

    def __init__(self, ap, bufs):
        self.ap = ap
        self.bufs = bufs if isinstance(bufs, (list, tuple)) else [bufs]

    def __getitem__(self, idx):
        return T(self.ap[idx], self.bufs)

    def bitcast(self, dt):
        return T(self.ap.bitcast(dt), self.bufs)

    def rearrange(self, pat, **kw):
        return T(self.ap.rearrange(pat, **kw), self.bufs)

    def bc(self, shape):
        return T(self.ap.broadcast_to(shape), self.bufs)

    def on(self, *bufs):
        return T(self.ap, list(bufs))


class Op:
    __slots__ = ("eng", "fn", "deps", "pos", "need_inc", "inc_val", "is_dma", "sem", "sem_val", "name")


ENGS = ["pe", "act", "dve", "pool", "sp"]
POOL_DMA_INFLIGHT = 4
SAME_ENG_WIN = 8


class Sched:
    def __init__(self, nc):
        self.nc = nc
        self.ops = {e: [] for e in ENGS}
        self.barrier_deps = []
        self.dma_sem_of = {}
        self.dma_sem_cnt = []
        self.dma_last = []
        self.final_dma = []
        self.pool_dma_hist = []

    def add(self, eng, fn, reads, writes, dma_key=None, name=""):
        op = Op()
        op.eng = eng
        op.fn = fn
        op.name = name
        op.pos = len(self.ops[eng])
        op.need_inc = False
        op.inc_val = 0
        op.is_dma = dma_key is not None
        op.sem = None
        op.sem_val = 0
        deps = set(self.barrier_deps)
        rb = []
        for t in reads:
            if isinstance(t, T):
                rb.extend(t.bufs)
        wb = []
        for t in writes:
            wb.extend(t.bufs)
        for b in rb:
            if b.last_w is not None:
                deps.add(b.last_w)
        for b in wb:
            if b.last_w is not None:
                deps.add(b.last_w)
            deps.update(b.readers)
        op.deps = deps
        for b in rb:
            b.readers.append(op)
        for b in wb:
            b.last_w = op
            b.readers = []
        if op.is_dma:
            dma_key = (eng, dma_key)
            if dma_key not in self.dma_sem_of:
                self.dma_sem_of[dma_key] = len(self.dma_sem_cnt)
                self.dma_sem_cnt.append(0)
                self.dma_last.append(None)
            si = self.dma_sem_of[dma_key]
            self.dma_sem_cnt[si] += 1
            op.sem = si
            op.sem_val = 16 * self.dma_sem_cnt[si]
            self.dma_last[si] = op
        self.ops[eng].append(op)
        return op

    def barrier(self):
        deps = []
        for e in ENGS:
            if self.ops[e]:
                deps.append(self.ops[e][-1])
        for op in self.dma_last:
            if op is not None:
                deps.append(op)
        self.barrier_deps = deps

    @staticmethod
    def _a(x):
        return x.ap if isinstance(x, T) else x

    def mm(self, out, lhsT, rhs, start=True, stop=True):
        nc = self.nc
        return self.add("pe", lambda: nc.tensor.matmul(out.ap, lhsT.ap, rhs.ap, start=start, stop=stop),
                        [lhsT, rhs], [out], name="mm")

    def tr(self, out, in_, ident):
        nc = self.nc
        return self.add("pe", lambda: nc.tensor.transpose(out.ap, in_.ap, ident.ap), [in_, ident], [out], name="tr")

    def act(self, out, in_, func, scale=1.0, bias=None, eng="act"):
        nc = self.nc
        a = self._a
        kw = {}
        if bias is not None:
            kw["bias"] = a(bias)
        return self.add("act", lambda: nc.scalar.activation(out.ap, in_.ap, func, scale=a(scale), **kw),
                        [in_, scale, bias], [out], name="act")

    def _e(self, eng):
        return self.nc.vector if eng == "dve" else self.nc.gpsimd

    def ts(self, eng, out, in0, s1, op0, s2=None, op1=None):
        e = self._e(eng)
        a = self._a
        if op1 is None:
            fn = lambda: e.tensor_scalar(out.ap, in0.ap, a(s1), None, op0)
        else:
            fn = lambda: e.tensor_scalar(out.ap, in0.ap, a(s1), a(s2), op0, op1)
        return self.add(eng, fn, [in0, s1, s2], [out], name="ts")

    def tt(self, eng, out, in0, in1, op):
        e = self._e(eng)
        return self.add(eng, lambda: e.tensor_tensor(out.ap, in0.ap, in1.ap, op), [in0, in1], [out], name="tt")

    def stt(self, out, in0, scalar, in1, op0, op1):
        nc = self.nc
        a = self._a
        return self.add("dve", lambda: nc.vector.scalar_tensor_tensor(out.ap, in0.ap, a(scalar), in1.ap, op0, op1),
                        [in0, scalar, in1], [out], name="stt")

    def copy(self, eng, out, in_):
        if eng == "act":
            return self.act(out, in_, AF.Copy)
        e = self._e(eng)
        return self.add(eng, lambda: e.tensor_copy(out.ap, in_.ap), [in_], [out], name="copy")

    def recip(self, out, in_):
        nc = self.nc
        return self.add("dve", lambda: nc.vector.reciprocal(out.ap, in_.ap), [in_], [out], name="recip")

    def memset(self, eng, out, val):
        e = self._e(eng)
        return self.add(eng, lambda: e.memset(out.ap, val), [], [out], name="memset")

    def bn_stats(self, out, in_):
        nc = self.nc
        return self.add("dve", lambda: nc.vector.bn_stats(out.ap, in_.ap), [in_], [out], name="bns")

    def bn_aggr(self, out, in_):
        nc = self.nc
        return self.add("dve", lambda: nc.vector.bn_aggr(out.ap, in_.ap), [in_], [out], name="bna")

    def scan(self, out, d0, d1, init, op0, op1):
        nc = self.nc
        return self.add("dve", lambda: nc.vector.tensor_tensor_scan(out.ap, d0.ap, d1.ap, init, op0, op1),
                        [d0, d1], [out], name="scan")

    def dma(self, q, out, in_, key, final=False, **kw):
        nc = self.nc
        e = {"sp": nc.sync, "pool": nc.gpsimd, "act": nc.scalar}[q]
        op = self.add(q, lambda: e.dma_start(out=out.ap, in_=in_.ap, **kw), [in_], [out], dma_key=key, name="dma")
        if q == "pool":
            hist = self.pool_dma_hist
            if len(hist) >= POOL_DMA_INFLIGHT:
                op.deps.add(hist[-POOL_DMA_INFLIGHT])
            hist.append(op)
        if final:
            self.final_dma.append(op)
        return op

    def emit(self, block_engines, sems_compute, sems_dma):
        for e in ENGS:
            for op in self.ops[e]:
                for d in op.deps:
                    if d.is_dma:
                        continue
                    if d.eng != op.eng:
                        d.need_inc = True
                    elif op.eng != "pe" and (op.pos - d.pos) <= SAME_ENG_WIN:
                        d.need_inc = True
        for e in ENGS:
            cnt = 0
            for op in self.ops[e]:
                if op.is_dma:
                    continue
                if op.need_inc:
                    cnt += 1
                    op.inc_val = cnt
        for e in ENGS:
            eng = block_engines[e]
            waited = {}
            ops = self.ops[e]
            for op in ops:
                need = {}
                for d in op.deps:
                    if d.is_dma:
                        key = ("d", d.sem)
                        val = d.sem_val
                    else:
                        if d.eng == e and (e == "pe" or (op.pos - d.pos) > SAME_ENG_WIN):
                            continue
                        key = ("c", d.eng)
                        val = d.inc_val
                    if val > need.get(key, 0):
                        need[key] = val
                for key, val in need.items():
                    if waited.get(key, 0) >= val:
                        continue
                    waited[key] = val
                    sem = sems_dma[key[1]] if key[0] == "d" else sems_compute[key[1]]
                    eng.wait_ge(sem, val)
                ins = op.fn()
                if op.is_dma:
                    ins.then_inc(sems_dma[op.sem], 16)
                elif op.need_inc:
                    ins.then_inc(sems_compute[e], 1)
            if e == "sp":
                need = {}
                for d in self.final_dma:
                    need[d.sem] = max(need.get(d.sem, 0), d.sem_val)
                for si, val in need.items():
                    eng.wait_ge(sems_dma[si], val)


def _t5_bucket_np(dist):
    dist = np.asarray(dist, dtype=np.int32)
    d = np.maximum(dist, 1).astype(np.float32)
    large = 16 + (np.log(d / np.float32(16.0)) / np.float32(math.log(8.0)) * np.float32(16.0)).astype(np.int32)
    large = np.minimum(large, 31)
    return np.where(dist < 16, dist, large)


def make_consts():
    ident = np.eye(128, dtype=np.float32)
    tri = (np.arange(128)[:, None] <= np.arange(128)[None, :]).astype(np.float32)
    scanmask = np.ones((128, 256), np.float32)
    scanmask[:, 0] = 0.0
    scanmask[:, 128] = 0.0
    ohg = np.zeros((32, 384), np.float32)
    j = np.arange(127, 383)
    ohg[_t5_bucket_np(j - 127), j] = 1.0
    rev = np.ascontiguousarray(ident[::-1])
    return {"c_ident": ident, "c_tri": tri, "c_scanmask": scanmask, "c_ohg": ohg, "c_rev": rev}


def build(S=4096, NSEQ=2, mode="full"):
    nc = bass.Bass("TRN2", target_bir_lowering=False)
    TOK = NSEQ * S
    NST = S // 256
    sch = Sched(nc)

    def din(name, shape, dt=F32):
        return T(nc.dram_tensor(name, list(shape), dt, kind="ExternalInput").ap(), Buf(name))

    x_d = din("x", [TOK, D])
    c_d = din("c", [NSEQ, D])
    wada_d = din("w_ada", [D, 6 * D])
    bada_d = din("b_ada", [48, 128])
    win_d = din("w_in", [D, INW])
    lbl_d = din("lb_logits", [8, 128])
    hnw_d = din("hgrn_norm_w", [128, 1])
    lam_d = [din(n, [1, 64]) for n in ("lam_q1", "lam_k1", "lam_q2", "lam_k2")]
    dnw_d = din("diff_norm_w", [128, 1])
    relb_d = din("rel_bias", [32, 4])
    wout_d = din("w_out", [D, D])
    ln1g_d = din("ln1_g", [1, D])
    ln1b_d = din("ln1_b", [1, D])
    wg_d = din("w_gate", [D, DFF])
    wu_d = din("w_up", [D, DFF])
    wd_d = din("w_down", [DFF, D])
    ln2g_d = din("ln2_g", [1, D])
    ln2b_d = din("ln2_b", [1, D])
    cid_d = din("c_ident", [128, 128])
    ctri_d = din("c_tri", [128, 128])
    csm_d = din("c_scanmask", [128, 256])
    cohg_d = din("c_ohg", [32, 384])
    crev_d = din("c_rev", [128, 128])
    out_d = T(nc.dram_tensor("out", [TOK, D], F32, kind="ExternalOutput").ap(), Buf("out"))
    x1s_d = T(nc.dram_tensor("x1s", [TOK, D], F32, kind="Internal").ap(), Buf("x1s"))
    gscr_d = T(nc.dram_tensor("gscr", [4, 384], F32, kind="Internal").ap(), Buf("gscr"))

    import contextlib
    es = contextlib.ExitStack()
    with es:
        ARENA_W = 52992
        arena = es.enter_context(nc.sbuf_tensor("arena", [128, ARENA_W], F32))
        psum = [es.enter_context(nc.psum_tensor(f"ps{i}", [128, 512], F32)) for i in range(8)]
        PS = [T(psum[i][:, :], Buf(f"ps{i}")) for i in range(8)]

        class Alloc:
            def __init__(self, base, limit):
                self.off = base
                self.limit = limit

            def take(self, name, words, dt=F32, shape=None, buf=None):
                a = self.off
                self.off += (words + 7) // 8 * 8
                assert self.off <= self.limit, (name, self.off, self.limit)
                ap = arena[:, a:a + words]
                if dt == BF16:
                    ap = ap.bitcast(BF16)
                t = T(ap, buf if buf is not None else Buf(name))
                if shape is not None:
                    names = " ".join(f"d{i}" for i in range(len(shape)))
                    kw = {f"d{i}": s for i, s in enumerate(shape)}
                    t = t.rearrange(f"p ({names}) -> p {names}", **kw)
                return t

        AP_ = Alloc(0, ARENA_W)
        ident_f = AP_.take("ident_f", 128, shape=None)
        ident_b = AP_.take("ident_b", 64, BF16)
        ones_b = AP_.take("ones_b", 64, BF16)
        tri_b = AP_.take("tri_b", 64, BF16)
        zeros_f = AP_.take("zeros_f", 128)
        adaT = AP_.take("adaT", 48 * NSEQ, shape=[48, NSEQ])
        scl = AP_.take("scl", 16 * NSEQ, shape=[2, 8, NSEQ])
        smallv = AP_.take("smallv", 64)
        gbc = AP_.take("gbc", D)
        bbc = AP_.take("bbc", D)
        PERSIST_END = AP_.off

        lbT = smallv[:, 0:4]
        hnw = smallv[:, 8:9]
        dnw8 = smallv[:, 9:10]
        neglam = smallv[:, 10:11]
        crel = smallv[:, 11:12]
        lamt = smallv[:, 12:16]

        bank_rr = [0]

        def nextbank(pool):
            b = pool[bank_rr[0] % len(pool)]
            bank_rr[0] += 1
            return PS[b]

        def ln_stats(src, st6, mv, sm):
            sch.bn_stats(st6[:, 0:6], src[:, 0:512])
            sch.bn_stats(st6[:, 6:12], src[:, 512:1024])
            sch.bn_aggr(mv, st6)
            sch.act(sm[:, 2:3], mv[:, 1:2], AF.Ln, scale=1.0, bias=epsln)
            sch.act(sm[:, 0:1], sm[:, 2:3], AF.Exp, scale=-0.5)
            sch.stt(sm[:, 1:2], mv[:, 0:1], -1.0, sm[:, 0:1], ALU.mult, ALU.mult)

        def ln_mod_T(xts, xhat, U, sidx, which, banks, st6, mv, sm):
            pa, pb = banks
            pab = [pa.bitcast(BF16).rearrange("p (k t) -> p k t", k=4),
                   pb.bitcast(BF16).rearrange("p (k t) -> p k t", k=4)]
            for j in range(2):
                ln_stats(xts[j], st6[j], mv[j], sm[j])
                sch.act(xhat[:, j, :], xts[j], AF.Identity, scale=sm[j][:, 0:1], bias=sm[j][:, 1:2])
                for kc in range(8):
                    sch.tr(pab[kc // 4][:, kc % 4, j * 128:(j + 1) * 128], xhat[:, j, kc * 128:(kc + 1) * 128], ident_b)
            shift_base = 0 if which == 0 else 24
            for kc in range(8):
                src = pab[kc // 4][:, kc % 4, :]
                s_ap = scl[:, which, kc, sidx:sidx + 1]
                b_ap = adaT[:, shift_base + kc, sidx:sidx + 1]
                if kc % 2 == 0:
                    sch.act(U[:, kc, :], src, AF.Identity, scale=s_ap, bias=b_ap)
                else:
                    sch.ts("dve", U[:, kc, :], src, s_ap, ALU.mult, b_ap, ALU.add)

        def ln_final(t2, xn, ob, st6, mv, sm):
            ln_stats(t2, st6, mv, sm)
            sch.act(xn, t2, AF.Identity, scale=sm[:, 0:1], bias=sm[:, 1:2])
            sch.tt("pool", ob, xn, gbc, ALU.mult)
            sch.tt("pool", ob, ob, bbc, ALU.add)

        def make_bc_from_ada(dst, chunk0, sidx, plus_one, banks):
            for j in range(8):
                pb = nextbank(banks)
                rep = reptile
                sch.ts("dve", rep, zeros_f, adaT[:, chunk0 + j, sidx:sidx + 1], ALU.add,
                       1.0 if plus_one else 0.0, ALU.add)
                sch.tr(pb[:, 0:128], rep, ident_f)
                sch.copy("dve", dst[:, j * 128:(j + 1) * 128], pb[:, 0:128])

        P0 = Alloc(PERSIST_END, ARENA_W)
        epsln = P0.take("epsln", 8)
        sch.memset("dve", epsln[:, 0:1], LN_EPS)
        sch.memset("dve", epsln[:, 1:2], RMS_EPS)
        sch.memset("dve", epsln[:, 2:3], 1.0)
        epsrms = epsln[:, 1:2]
        onec = epsln[:, 2:3]
        epsln = epsln[:, 0:1]
        reptile = P0.take("reptile", 128)
        WIN_WORDS = 8 * INW // 2
        win = None
        if mode in ("full", "mixer"):
            win = P0.take("win", WIN_WORDS, BF16, shape=[8, INW])
            for kc in range(8):
                sch.dma("pool", win[:, kc, :], win_d[kc * 128:(kc + 1) * 128, :], key=Buf(f"winld{kc}"),
                        max_dma_last_dim=4096)
        stage = P0.take("stage", 384)
        sch.dma("sp", ident_f, cid_d, key=ident_f.bufs[0])
        sch.copy("dve", ident_b, ident_f)
        sch.memset("dve", ones_b, 1.0)
        sch.memset("dve", zeros_f, 0.0)
        sch.dma("sp", stage[:, 0:128], ctri_d, key=stage.bufs[0])
        sch.copy("dve", tri_b, stage[:, 0:128])
        sch.dma("sp", hnw, hnw_d, key=Buf("hnwld"))
        dnw_raw = P0.take("dnw_raw", 8)
        sch.dma("sp", dnw_raw[:, 0:1], dnw_d, key=dnw_raw.bufs[0])
        sch.ts("dve", dnw8, dnw_raw[:, 0:1], 1.0 - LAM_INIT, ALU.mult)
        lamv = P0.take("lamv", 256, shape=[4, 64])
        for i in range(4):
            sch.dma("sp", lamv[:, i, :], T(lam_d[i].ap.partition_broadcast(128), lam_d[i].bufs), key=Buf(f"lam{i}"))
        lprod = P0.take("lprod", 128, shape=[2, 64])
        sch.tt("dve", lprod[:, 0, :], lamv[:, 0, :], lamv[:, 1, :], ALU.mult)
        sch.tt("dve", lprod[:, 1, :], lamv[:, 2, :], lamv[:, 3, :], ALU.mult)
        nc_v = nc.vector
        sch.add("dve", lambda: nc_v.tensor_reduce(lamt[:, 0:2].ap, lprod.ap, mybir.AxisListType.X, ALU.add),
                [lprod], [lamt], name="red")
        sch.act(lamt[:, 2:4], lamt[:, 0:2], AF.Exp)
        sch.tt("dve", neglam, lamt[:, 3:4], lamt[:, 2:3], ALU.subtract)
        sch.ts("dve", neglam, neglam, -LAM_INIT, ALU.add)
        lbrow = P0.take("lbrow", 128)
        sch.dma("sp", lbrow[0:8, :], lbl_d, key=lbrow.bufs[0])
        pb = nextbank([4, 5, 6, 7])
        sch.tr(pb[:, 0:8], lbrow[0:8, :], ident_f[0:8, 0:8])
        lbtmp = P0.take("lbtmp", 8)
        sch.copy("dve", lbtmp[:, 0:8], pb[:, 0:8])
        sch.tt("dve", lbtmp[:, 0:4], lbtmp[:, 0:4], lbtmp[:, 4:8], ALU.subtract)
        sch.act(lbtmp[:, 4:8], lbtmp[:, 0:4], AF.Exp, scale=-1.0)
        sch.ts("dve", lbtmp[:, 4:8], lbtmp[:, 4:8], 1.0, ALU.add)
        sch.recip(lbT, lbtmp[:, 4:8])
        crow = P0.take("crow", D)
        sch.dma("sp", crow[0:NSEQ, :], c_d, key=crow.bufs[0])
        cwork = P0.take("cwork", D)
        sch.act(cwork[0:NSEQ, :], crow[0:NSEQ, :], AF.Exp, scale=-1.0)
        sch.ts("dve", cwork[0:NSEQ, :], cwork[0:NSEQ, :], 1.0, ALU.add)
        sch.recip(cwork[0:NSEQ, :], cwork[0:NSEQ, :])
        sch.tt("dve", crow[0:NSEQ, :], crow[0:NSEQ, :], cwork[0:NSEQ, :], ALU.mult)
        scT = P0.take("scT", 8 * NSEQ, BF16, shape=[8, 2 * NSEQ])[:, :, 0:NSEQ]
        pb = nextbank([4, 5, 6, 7])
        for kc in range(8):
            sch.tr(pb[:, kc * NSEQ:(kc + 1) * NSEQ], crow[0:NSEQ, kc * 128:(kc + 1) * 128], ident_f[0:NSEQ, 0:NSEQ])
        sch.copy("dve", scT, pb[:, 0:8 * NSEQ].rearrange("p (k s) -> p k s", s=NSEQ))
        badaT = P0.take("badaT", 48)
        brow = P0.take("brow", 128)
        sch.dma("sp", brow[0:48, :], bada_d, key=brow.bufs[0])
        pb = nextbank([4, 5, 6, 7])
        sch.tr(pb[:, 0:48], brow[0:48, :], ident_f[0:48, 0:48])
        sch.copy("dve", badaT, pb[:, 0:48])
        wst = [P0.take(f"wada{i}", 8 * D // 2, BF16, shape=[8, D]) for i in range(3)]
        for piece in range(6):
            w = wst[piece % 3]
            sch.dma("pool", w, T(wada_d.ap[:, piece * D:(piece + 1) * D].rearrange("(k p) n -> p k n", p=128), wada_d.bufs),
                    key=w.bufs[0], max_dma_last_dim=4096)
            pb = nextbank([4, 5, 6, 7])
            for j in range(8):
                for kc in range(8):
                    sch.mm(pb[:, j * NSEQ:(j + 1) * NSEQ], w[:, kc, j * 128:(j + 1) * 128], scT[:, kc, :],
                           start=(kc == 0), stop=(kc == 7))
            for s_ in range(NSEQ):
                sch.tt("dve", adaT[:, piece * 8:(piece + 1) * 8, s_],
                       pb[:, 0:8 * NSEQ].rearrange("p (j s) -> p j s", s=NSEQ)[:, :, s_],
                       badaT[:, piece * 8:(piece + 1) * 8], ALU.add)
        sch.ts("dve", scl[:, 0, :, :], adaT[:, 8:16, :], 1.0, ALU.add)
        sch.ts("dve", scl[:, 1, :, :], adaT[:, 32:40, :], 1.0, ALU.add)

        if mode in ("full", "mixer"):
            sch.barrier()
            A1 = Alloc(PERSIST_END, ARENA_W)
            epsln_k = A1.take("eps_keep", 8)
            reptile = A1.take("reptile1", 128)
            A1.take("win_keep", WIN_WORDS)
            wout = A1.take("wout", 8 * D // 2, BF16, shape=[8, D])
            kT = A1.take("kT", 4 * S // 2, BF16, shape=[4, S])
            Vc = A1.take("Vc", (S // 128) * 512 // 2, BF16, shape=[S // 128, 512])
            xt_buf = [Buf("xt0"), Buf("xt1")]
            xtile = A1.take("xt", 2 * D, shape=[2, D])
            xts = [xtile[:, j, :].on(xt_buf[j]) for j in range(2)]
            U = A1.take("U", 8 * 256 // 2, BF16, shape=[8, 256])
            qpad = A1.take("qpad", 4 * 2 * 256 // 2, BF16, shape=[4, 2, 256])
            G = [A1.take(f"G{i}", D) for i in range(5)]
            qhat = A1.take("qhat", 512, BF16, shape=[4, 256])
            khat = A1.take("khat", 512, BF16, shape=[4, 256])
            gate = A1.take("gate", 512, BF16, shape=[4, 256])
            vh = A1.take("vh", 512, BF16, shape=[2, 512])
            Am = A1.take("Am", 256, BF16, shape=[4, 128])
            khT = A1.take("khT", 256, BF16, shape=[4, 128])
            Sbf = A1.take("Sbf", 256, BF16, shape=[4, 128])
            Sst = A1.take("Sst", 512, shape=[4, 128])
            PT = [A1.take(f"PT{i}", 256, BF16, shape=[2, 256]) for i in range(2)]
            biasT = A1.take("biasT", 512, BF16, shape=[4, 256])
            scanmask = A1.take("scanmask", 256)
            smt = A1.take("smt", 64)
            lnsm = [A1.take(f"lnsm{i}", 24) for i in range(4)]
            rlb = A1.take("rlb", 512, shape=[2, 256])
            odb = A1.take("odb", 256)
            sqb = A1.take("sqb", 128, BF16)
            rsb = A1.take("rsb", 256)
            print("phase1 arena words used", A1.off, "of", ARENA_W)

            BK = [6, 2, 3, 7, 0, 1, 4, 5]
            BKB = [6]
            BKP = [7]

            sch.dma("sp", scanmask, csm_d, key=scanmask.bufs[0])
            relb = G[0][0:32, 0:4]
            sch.dma("sp", relb, relb_d, key=Buf("relbld"))
            ohg_c = G[1][0:32, 0:384]
            sch.dma("sp", ohg_c, cohg_d, key=Buf("ohgld"))
            sch.dma("sp", crel[0:4, :], T(relb_d.ap[31:32, :].rearrange("o h -> h o"), relb_d.bufs), key=Buf("crelld"))
            pb = nextbank(BK)
            sch.mm(pb[0:4, 0:384], relb, ohg_c)
            grow = G[2][0:4, 0:384]
            sch.ts("dve", grow, pb[0:4, 0:384], crel[0:4, :], ALU.subtract, 8.0, ALU.mult)
            sch.memset("dve", grow[:, 0:127], MASKVAL)
            sch.dma("sp", gscr_d, grow, key=Buf("gscrst"))
            btf = G[3].rearrange("p (h j) -> p h j", h=4)
            rev_f = G[0][:, 512:640]
            sch.dma("sp", rev_f, crev_d, key=Buf("revld"))
            for h in range(4):
                src = bass.AP(tensor=gscr_d.ap.tensor, offset=h * 384, ap=[[1, 128], [1, 256]])
                sch.dma("sp", btf[:, h, :], T(src, gscr_d.bufs), key=Buf(f"btf{h}"))
                pb = nextbank(BK)
                sch.mm(pb[:, 0:256], rev_f, btf[:, h, :])
                sch.copy("dve", biasT[:, h, :], pb[:, 0:256])
            sch.memset("pool", qpad, 0.0)

            def load_wout(sidx):
                make_bc_from_ada(G[4], 16, sidx, True, BK)
                for kc in range(8):
                    stg = G[kc % 4]
                    sch.dma("sp", stg, wout_d[kc * 128:(kc + 1) * 128, :], key=stg.bufs[0])
                    sch.tt("dve", wout[:, kc, :], stg, G[4], ALU.mult)

            sch.dma("sp", gbc, T(ln1g_d.ap.partition_broadcast(128), ln1g_d.bufs), key=gbc.bufs[0])
            sch.dma("sp", bbc, T(ln1b_d.ap.partition_broadcast(128), ln1b_d.bufs), key=bbc.bufs[0])

            st6 = [lnsm[j][:, 0:12] for j in range(4)]
            mv = [lnsm[j][:, 12:14] for j in range(4)]
            sm = [lnsm[j][:, 16:20] for j in range(4)]
            items = [(sidx, st) for sidx in range(NSEQ) for st in range(NST)]
            NI = len(items)
            xhat = G[4].bitcast(BF16).rearrange("p (j d) -> p j d", j=2)
            uT = G[4].bitcast(BF16).rearrange("p (k t) -> p k t", k=8)
            E, L1, L2, hqs = G[0], G[1], G[2], G[3]
            E4 = E.rearrange("p (h t) -> p h t", h=4)
            L14 = L1.rearrange("p (h t) -> p h t", h=4)
            L24 = L2.rearrange("p (h t) -> p h t", h=4)
            hq4 = hqs.rearrange("p (h t) -> p h t", h=4)

            def A_pre(i):
                sidx, st = items[i]
                tok0 = sidx * S + st * 256
                for j in range(2):
                    sch.dma("sp", xts[j], x_d[tok0 + j * 128: tok0 + (j + 1) * 128, :], key=xt_buf[j])
                    yield
                for j in range(2):
                    ln_stats(xts[j], st6[j], mv[j], sm[j])
                    yield
                    sch.act(xhat[:, j, :], xts[j], AF.Identity, scale=sm[j][:, 0:1], bias=sm[j][:, 1:2])
                    yield

            def A_tr(i):
                sidx, st = items[i]
                pa, pb = PS[0], PS[1]
                bank_rr[0] = 0
                pab = [pa.bitcast(BF16).rearrange("p (k t) -> p k t", k=4),
                       pb.bitcast(BF16).rearrange("p (k t) -> p k t", k=4)]
                for j in range(2):
                    for kc in range(8):
                        sch.tr(pab[kc // 4][:, kc % 4, j * 128:(j + 1) * 128], xhat[:, j, kc * 128:(kc + 1) * 128], ident_b)
                    yield
                for kc in range(8):
                    src = pab[kc // 4][:, kc % 4, :]
                    s_ap = scl[:, 0, kc, sidx:sidx + 1]
                    b_ap = adaT[:, kc, sidx:sidx + 1]
                    if kc % 2 == 0:
                        sch.act(uT[:, kc, :], src, AF.Identity, scale=s_ap, bias=b_ap)
                    else:
                        sch.ts("dve", uT[:, kc, :], src, s_ap, ALU.mult, b_ap, ALU.add)
                    if kc % 2 == 1:
                        yield

            def A_pre_tr(i):
                yield from A_pre(i)
                yield from A_tr(i)

            def A_proj(i):
                sidx, st = items[i]
                q0 = st * 256

                def fm_group(col0, h0):
                    pb = nextbank(BK)
                    pv = pb.rearrange("p (h t) -> p h t", h=2)
                    for hh in range(2):
                        c0 = col0 + (h0 + hh) * 128
                        for kc in range(8):
                            sch.mm(pv[:, hh, :], win[:, kc, c0:c0 + 128], uT[:, kc, :], start=(kc == 0), stop=(kc == 7))
                    return pv

                for h0 in (0, 2):
                    pv = fm_group(1536, h0)
                    tmpg = L24[:, h0:h0 + 2, :]
                    sch.act(tmpg, pv, AF.Exp, scale=-1.0)
                    sch.act(tmpg, tmpg, AF.Ln, scale=1.0, bias=onec)
                    sch.act(tmpg, tmpg, AF.Exp, scale=-1.0)
                    sch.tt("dve", gate[:, h0:h0 + 2, :], pv, tmpg, ALU.mult)
                for h0 in (0, 2):
                    pv = fm_group(512, h0)
                    sch.act(E4[:, h0:h0 + 2, :], pv, AF.Exp, scale=-1.0)
                for h0 in (0, 2):
                    pv = fm_group(0, h0)
                    sch.copy("dve", hq4[:, h0:h0 + 2, :], pv)
                for j in range(2):
                    for (col0, dst) in ((1024, vh[:, j, :]), (3072, Vc[:, st * 2 + j, :])):
                        pb = nextbank(BK)
                        for kc in range(8):
                            sch.mm(pb, uT[:, kc, j * 128:(j + 1) * 128], win[:, kc, col0:col0 + 512],
                                   start=(kc == 0), stop=(kc == 7))
                        sch.copy("act" if col0 == 1024 else "dve", dst, pb)
                for h0 in (0, 2):
                    pv = fm_group(2048, h0)
                    sch.copy("act", qpad[0:64, h0:h0 + 2, 0, :], pv[0:64, :, :])
                    sch.copy("dve", qpad[64:128, h0:h0 + 2, 1, :], pv[64:128, :, :])
                for h0 in (0, 2):
                    pv = fm_group(2560, h0)
                    sch.copy("act", kT[:, h0:h0 + 2, q0:q0 + 256], pv)

            sm8 = smt[:, 0:16].rearrange("p (g two) -> p g two", two=2)
            ebm = smt[:, 16:24]
            d1 = smt[:, 24:32]
            d2 = smt[:, 32:40]

            def B_pre(i):
                sch.act(L2, E, AF.Ln, scale=1.0, bias=onec)
                yield
                for h in range(4):
                    sch.act(L14[:, h, :], E4[:, h, :], AF.Ln, scale=lbT[:, h:h + 1], bias=onec)
                yield
                sch.tt("dve", L1, L1, L2, ALU.subtract)
                yield
                sch.act(E, L1, AF.Exp)
                yield
                sch.act(E, E, AF.Identity, scale=-1.0, bias=onec)
                for h in range(4):
                    sch.scan(L24[:, h, :], scanmask, L14[:, h, :], 0.0, ALU.mult, ALU.add)
                    yield
                b8 = L2.rearrange("p (g t) -> p g t", t=128)
                sch.copy("dve", sm8, b8[:, :, 63:128:64])
                yield
                sch.tt("dve", b8, b8, sm8[:, :, 0:1].bc([128, 8, 128]), ALU.subtract)
                sch.tt("dve", d2, sm8[:, :, 1], sm8[:, :, 0], ALU.subtract)
                yield
                sch.act(L1, L2, AF.Exp)
                yield
                sch.act(L2, L2, AF.Exp, scale=-1.0)
                sch.act(ebm, sm8[:, :, 0], AF.Exp)
                sch.act(d1, sm8[:, :, 1], AF.Exp)
                sch.act(d2, d2, AF.Exp)
                yield
                sch.stt(qhat, hq4, 128.0 ** -0.5, L14, ALU.mult, ALU.mult)
                yield
                sch.tt("dve", khat[:, 0:2, :], E4[:, 0:2, :], L24[:, 0:2, :], ALU.mult)
                sch.tt("pool", khat[:, 2:4, :], E4[:, 2:4, :], L24[:, 2:4, :], ALU.mult)
                yield

            def B_gen(i):
                ohg = G[0].rearrange("p (h t) -> p h t", h=4)
                for cch in range(2):
                    tsl = slice(cch * 128, (cch + 1) * 128)
                    pA = nextbank(BKB)
                    pA4 = pA.rearrange("p (h t) -> p h t", h=4)
                    for h in range(4):
                        sch.mm(pA4[:, h, :], khat[:, h, tsl], qhat[:, h, tsl])
                    yield
                    sch.tt("dve", Am, pA4, tri_b.rearrange("p (o t) -> p o t", o=1).bc([128, 4, 128]), ALU.mult)
                    pK = nextbank(BKB)
                    pK4 = pK[:, 0:256].bitcast(BF16).rearrange("p (h t) -> p h t", h=4)
                    for h in range(4):
                        sch.tr(pK4[:, h, :], khat[:, h, tsl], ident_b)
                    yield
                    sch.copy("dve", khT, pK4)
                    g8 = lambda v: v.rearrange("p (h c) -> p h c", c=2)[:, :, cch:cch + 1].bc([128, 4, 128])
                    sch.tt("dve", Sbf, Sst, g8(ebm), ALU.mult)
                    yield
                    pO = nextbank(BKB)
                    pO4 = pO.rearrange("p (h t) -> p h t", h=4)
                    for h in range(4):
                        sch.mm(pO4[:, h, :], vh[:, cch, h * 128:(h + 1) * 128], Am[:, h, :], start=True, stop=False)
                        sch.mm(pO4[:, h, :], Sbf[:, h, :], qhat[:, h, tsl], start=False, stop=True)
                    yield
                    sch.copy("dve", ohg[:, :, tsl], pO4)
                    pP = nextbank(BKB)
                    pP4 = pP.rearrange("p (h t) -> p h t", h=4)
                    for h in range(4):
                        sch.mm(pP4[:, h, :], khT[:, h, :], vh[:, cch, h * 128:(h + 1) * 128])
                    yield
                    stmp = G[1][:, 0:512].rearrange("p (h t) -> p h t", h=4)
                    stmp2 = G[1][:, 512:1024].rearrange("p (h t) -> p h t", h=4)
                    sch.tt("pool", stmp, Sst, g8(d1), ALU.mult)
                    sch.tt("dve", stmp2, pP4, g8(d2), ALU.mult)
                    yield
                    sch.tt("dve", Sst, stmp, stmp2, ALU.add)
                    yield
                srcf = G[0]
                sq = G[2].bitcast(BF16)[:, 0:1024]
                sch.act(sq, srcf, AF.Square)
                yield
                rs = G[3]
                for half in range(2):
                    pS = nextbank(BKB)
                    sch.mm(pS, ones_b, sq[:, half * 512:(half + 1) * 512])
                    yield
                    sch.act(rs[:, half * 512:(half + 1) * 512], pS, AF.Ln, scale=1.0 / 128.0, bias=epsrms)
                    yield
                sch.act(rs, rs, AF.Exp, scale=-0.5)
                yield
                sch.tt("dve", rs, rs, srcf, ALU.mult)
                yield
                rs4 = rs.rearrange("p (h t) -> p h t", h=4)
                sch.stt(U[:, 0:4, :], rs4, hnw, gate, ALU.mult, ALU.mult)
                yield

            def C_attn(i, gens):
                sidx, st = items[i]
                q0 = st * 256
                qt0 = q0 // 128
                nkt = qt0 + 2
                live = list(gens)
                post_gen = [None]

                def bg():
                    for g in list(live):
                        try:
                            next(g)
                        except StopIteration:
                            live.remove(g)

                for h in range(4):
                    pO = PS[2 + 2 * (h % 2)]
                    pL = PS[3 + 2 * (h % 2)]
                    pOv = pO.rearrange("p (m t) -> p m t", m=2)
                    pLv = pL.rearrange("p (m t) -> p m t", m=2)

                    def qk(kt):
                        pS = PS[kt % 2]
                        pSv = pS.rearrange("p (m t) -> p m t", m=2)
                        c0 = 128 if kt == qt0 + 1 else 0
                        if kt == qt0 - 1:
                            win_q, win_b = (0, 128), (128, 256)
                        elif kt == qt0:
                            win_q, win_b = (0, 256), (0, 256)
                        elif kt == qt0 + 1:
                            win_q, win_b = (128, 256), (0, 128)
                        else:
                            win_q = None
                        for m in range(2):
                            sch.mm(pSv[:, m, c0:256], kT[:, h, kt * 128:(kt + 1) * 128], qpad[:, h, m, c0:256],
                                   start=True, stop=(win_q is None))
                            if win_q is not None:
                                sch.mm(pSv[:, m, win_q[0]:win_q[1]], ident_b, biasT[:, h, win_b[0]:win_b[1]],
                                       start=False, stop=True)
                        pt = PT[kt % 2]
                        sch.act(pt[:, :, c0:256], pSv[:, :, c0:256], AF.Exp, scale=0.125)
                        return c0

                    def pv(kt, c0):
                        pt = PT[kt % 2]
                        sch.mm(pOv[:, :, c0:256], Vc[:, kt, h * 128:(h + 1) * 128], pt[:, :, c0:256],
                               start=(kt == 0), stop=(kt == nkt - 1))
                        sch.mm(pLv[:, :, c0:256], ones_b, pt[:, :, c0:256],
                               start=(kt == 0), stop=(kt == nkt - 1))

                    c0s = {0: qk(0)}
                    for kt in range(nkt):
                        if kt + 1 < nkt:
                            c0s[kt + 1] = qk(kt + 1)
                        pv(kt, c0s[kt])
                        bg()
                    if post_gen[0] is not None:
                        for _ in post_gen[0]:
                            pass
                        if post_gen[0] in live:
                            live.remove(post_gen[0])

                    def post(h=h, pOv=pOv, pLv=pLv):
                        sch.act(rlb, pLv, AF.Ln)
                        yield
                        sch.act(rlb, rlb, AF.Exp, scale=-1.0)
                        yield
                        sch.tt("dve", rlb, pOv, rlb, ALU.mult)
                        yield
                        sch.stt(odb, rlb[:, 1, :], neglam, rlb[:, 0, :], ALU.mult, ALU.add)
                        yield
                        sch.tt("pool", sqb, odb, odb, ALU.mult)
                        yield
                        yield
                        pS = nextbank(BKP)
                        sch.mm(pS[:, 0:256], ones_b, sqb)
                        yield
                        sch.act(rsb, pS[:, 0:256], AF.Ln, scale=1.0 / 128.0, bias=epsrms)
                        yield
                        sch.act(rsb, rsb, AF.Exp, scale=-0.5)
                        yield
                        sch.tt("dve", rsb, rsb, odb, ALU.mult)
                        yield
                        sch.ts("dve", U[:, 4 + h, :], rsb, dnw8, ALU.mult)

                    post_gen[0] = post()
                    live.insert(0, post_gen[0])
                while live:
                    bg()

            def D_ld(i, j):
                sidx, st = items[i]
                tok0 = sidx * S + st * 256
                t2s = xts
                sch.dma("sp", t2s[j], x_d[tok0 + j * 128: tok0 + (j + 1) * 128, :], key=xt_buf[j])

            def D_mm(i):
                sidx, st = items[i]
                tok0 = sidx * S + st * 256
                t2s = xts
                for j in range(2):
                    pM = [nextbank(BK), nextbank(BK)]
                    for n in range(2):
                        for kc in range(8):
                            sch.mm(pM[n], U[:, kc, j * 128:(j + 1) * 128], wout[:, kc, n * 512:(n + 1) * 512],
                                   start=(kc == 0), stop=(kc == 7))
                    t2 = t2s[j]
                    for n in range(2):
                        sch.stt(t2[:, n * 512:(n + 1) * 512], t2[:, n * 512:(n + 1) * 512], ALPHA, pM[n],
                                ALU.mult, ALU.add)

            def D_fin(i):
                sidx, st = items[i]
                tok0 = sidx * S + st * 256
                t2s = xts
                for j in range(2):
                    t2 = t2s[j]
                    ln_final(t2, t2, t2, st6[2 + j], mv[2 + j], sm[2 + j])
                    dst = x1s_d if mode == "full" else out_d
                    sch.dma("pool", dst[tok0 + j * 128: tok0 + (j + 1) * 128, :], t2, key=xt_buf[j],
                            final=(mode != "full"))

            def drain(g):
                for _ in g:
                    pass

            for i in range(NI):
                sidx, st = items[i]
                if st == 0:
                    load_wout(sidx)
                    sch.memset("pool", Sst, 0.0)
                    drain(A_pre_tr(i))
                    A_proj(i)
                    drain(B_pre(i))
                nxt_same = (i + 1 < NI) and items[i + 1][1] != 0
                def Bfull(i=i):
                    if st != 0:
                        yield from B_pre(i)
                    yield from B_gen(i)
                gens = [Bfull()]
                if nxt_same:
                    gens.append(A_pre(i + 1))
                C_attn(i, gens)
                D_ld(i, 0)
                D_ld(i, 1)
                if nxt_same:
                    drain(A_tr(i + 1))
                    A_proj(i + 1)
                D_mm(i)
                D_fin(i)

        if mode in ("full", "ffn"):
            sch.barrier()
            A2 = Alloc(PERSIST_END, ARENA_W)
            epsln_k = A2.take("eps_keep2", 8)
            reptile = A2.take("reptile2", 128)
            wg = A2.take("wg", 8 * DFF // 2, BF16, shape=[8, DFF])
            wu = A2.take("wu", 8 * DFF // 2, BF16, shape=[8, DFF])
            wd = A2.take("wd", NF * D // 2, BF16, shape=[NF, D])
            x1t = [[A2.take(f"x1t{s}{j}", D) for j in range(2)] for s in range(2)]
            xhat2 = A2.take("xhat2", D, BF16, shape=[2, D])
            U2 = [A2.take(f"U2{i}", 8 * 256 // 2, BF16, shape=[8, 256]) for i in range(2)]
            hT = A2.take("hT", NF * 256 // 2, BF16, shape=[NF, 256])
            hTf = [hT[:, f, :].on(Buf(f"hT{f}")) for f in range(NF)]
            sw = [[A2.take(f"sw{i}{k}", 256) for k in range(2)] for i in range(2)]
            tb = A2.take("tb", D)
            obb = [A2.take(f"obb{i}", D) for i in range(2)]
            gmbc = [A2.take(f"gmbc{i}", D) for i in range(NSEQ)]
            lnsm2 = [A2.take(f"lnsm2{i}", 24) for i in range(4)]
            print("phase2 arena words used", A2.off, "of", ARENA_W)
            BK2 = [0, 1, 2, 3, 4, 5, 6, 7]
            for kc in range(8):
                sch.dma("pool", wg[:, kc, :], wg_d[kc * 128:(kc + 1) * 128, :], key=Buf(f"wgld{kc}"), max_dma_last_dim=4096)
                sch.dma("pool", wu[:, kc, :], wu_d[kc * 128:(kc + 1) * 128, :], key=Buf(f"wuld{kc}"), max_dma_last_dim=4096)
            for f in range(NF):
                sch.dma("pool", wd[:, f, :], wd_d[f * 128:(f + 1) * 128, :], key=Buf(f"wdld{f}"), max_dma_last_dim=4096)
            sch.dma("sp", gbc, T(ln2g_d.ap.partition_broadcast(128), ln2g_d.bufs), key=gbc.bufs[0])
            sch.dma("sp", bbc, T(ln2b_d.ap.partition_broadcast(128), ln2b_d.bufs), key=bbc.bufs[0])
            for sidx in range(NSEQ):
                make_bc_from_ada(gmbc[sidx], 40, sidx, True, BK2)
            src_d = x1s_d if mode == "full" else x_d
            st6 = [lnsm2[j][:, 0:12] for j in range(4)]
            mv = [lnsm2[j][:, 12:14] for j in range(4)]
            sm = [lnsm2[j][:, 16:20] for j in range(4)]
            items = [(sidx, st) for sidx in range(NSEQ) for st in range(NST)]

            def p2_lnA(i):
                sidx, st = items[i]
                tok0 = sidx * S + st * 256
                xts2 = x1t[i % 2]
                for j in range(2):
                    sch.dma("sp", xts2[j], src_d[tok0 + j * 128: tok0 + (j + 1) * 128, :], key=xts2[j].bufs[0])
                for j in range(2):
                    ln_stats(xts2[j], st6[j], mv[j], sm[j])
                    sch.act(xhat2[:, j, :], xts2[j], AF.Identity, scale=sm[j][:, 0:1], bias=sm[j][:, 1:2])

            def p2_trB(i):
                sidx, st = items[i]
                pa, pb = nextbank(BK2), nextbank(BK2)
                pab = [pa.bitcast(BF16).rearrange("p (k t) -> p k t", k=4),
                       pb.bitcast(BF16).rearrange("p (k t) -> p k t", k=4)]
                for j in range(2):
                    for kc in range(8):
                        sch.tr(pab[kc // 4][:, kc % 4, j * 128:(j + 1) * 128], xhat2[:, j, kc * 128:(kc + 1) * 128], ident_b)
                U = U2[i % 2]
                for kc in range(8):
                    src = pab[kc // 4][:, kc % 4, :]
                    s_ap = scl[:, 1, kc, sidx:sidx + 1]
                    b_ap = adaT[:, 24 + kc, sidx:sidx + 1]
                    if kc % 2 == 0:
                        sch.act(U[:, kc, :], src, AF.Identity, scale=s_ap, bias=b_ap)
                    else:
                        sch.ts("dve", U[:, kc, :], src, s_ap, ALU.mult, b_ap, ALU.add)

            def p2_gu(i):
                U = U2[i % 2]
                for f in range(NF):
                    pb = nextbank(BK2)
                    pv = pb.rearrange("p (g t) -> p g t", g=2)
                    for gi, w in enumerate((wg, wu)):
                        for kc in range(8):
                            sch.mm(pv[:, gi, :], w[:, kc, f * 128:(f + 1) * 128], U[:, kc, :],
                                   start=(kc == 0), stop=(kc == 7))
                    ee, rr = sw[f % 2]
                    sch.act(ee, pv[:, 0, :], AF.Exp, scale=-1.0)
                    sch.act(ee, ee, AF.Ln, scale=1.0, bias=onec)
                    sch.act(rr, ee, AF.Exp, scale=-1.0)
                    sch.tt("dve", rr, pv[:, 0, :], rr, ALU.mult)
                    sch.tt("dve", hTf[f], rr, pv[:, 1, :], ALU.mult)

            def p2_downmm(i, j):
                pY = [nextbank(BK2), nextbank(BK2)]
                for n in range(2):
                    for f in range(NF):
                        sch.mm(pY[n], hTf[f][:, j * 128:(j + 1) * 128], wd[:, f, n * 512:(n + 1) * 512],
                               start=(f == 0), stop=(f == NF - 1))
                return pY

            def p2_fin(i, j, pY):
                sidx, st = items[i]
                tok0 = sidx * S + st * 256
                xts2 = x1t[i % 2]
                for n in range(2):
                    sch.tt("dve", tb[:, n * 512:(n + 1) * 512], pY[n], gmbc[sidx][:, n * 512:(n + 1) * 512], ALU.mult)
                sch.stt(tb, xts2[j], ALPHA, tb, ALU.mult, ALU.add)
                ob = obb[j]
                ln_final(tb, ob, ob, st6[2 + j], mv[2 + j], sm[2 + j])
                sch.dma("pool", out_d[tok0 + j * 128: tok0 + (j + 1) * 128, :], ob, key=ob.bufs[0], final=True)

            NI = len(items)
            p2_lnA(0)
            p2_trB(0)
            for i in range(NI):
                p2_gu(i)
                if i + 1 < NI:
                    p2_lnA(i + 1)
                pY0 = p2_downmm(i, 0)
                if i + 1 < NI:
                    p2_trB(i + 1)
                pY1 = p2_downmm(i, 1)
                p2_fin(i, 0, pY0)
                p2_fin(i, 1, pY1)

        nsem_dma = len(sch.dma_sem_cnt)
        print("ops:", {e: len(sch.ops[e]) for e in ENGS}, "dma sems:", nsem_dma)
        sems_c = {e: es.enter_context(nc.semaphore(f"sc_{e}")) for e in ENGS}
        sems_d = [es.enter_context(nc.semaphore(f"sd_{i}")) for i in range(nsem_dma)]
        block = es.enter_context(nc.Block())
        engs = {}

        @block.tensor
        def _(e):
            engs["pe"] = e
            sch.emit_one = None
            _emit_engine(sch, "pe", e, sems_c, sems_d)

        @block.scalar
        def _(e):
            _emit_engine(sch, "act", e, sems_c, sems_d)

        @block.vector
        def _(e):
            _emit_engine(sch, "dve", e, sems_c, sems_d)

        @block.gpsimd
        def _(e):
            _emit_engine(sch, "pool", e, sems_c, sems_d)

        @block.sync
        def _(e):
            _emit_engine(sch, "sp", e, sems_c, sems_d)
    return nc


_prepared = set()


def _prepare(sch):
    if id(sch) in _prepared:
        return
    _prepared.add(id(sch))
    for e in ENGS:
        for op in sch.ops[e]:
            for d in op.deps:
                if d.is_dma:
                    continue
                if d.eng != op.eng:
                    d.need_inc = True
                elif op.eng != "pe" and (op.pos - d.pos) <= SAME_ENG_WIN:
                    d.need_inc = True
    for e in ENGS:
        cnt = 0
        for op in sch.ops[e]:
            if op.is_dma:
                continue
            if op.need_inc:
                cnt += 1
                op.inc_val = cnt


def _emit_engine(sch, e, eng, sems_compute, sems_dma):
    _prepare(sch)
    waited = {}
    for op in sch.ops[e]:
        need = {}
        for d in op.deps:
            if d.is_dma:
                key = ("d", d.sem)
                val = d.sem_val
            else:
                if d.eng == e and (e == "pe" or (op.pos - d.pos) > SAME_ENG_WIN):
                    continue
                key = ("c", d.eng)
                val = d.inc_val
            if val > need.get(key, 0):
                need[key] = val
        for key, val in need.items():
            if waited.get(key, 0) >= val:
                continue
            waited[key] = val
            sem = sems_dma[key[1]] if key[0] == "d" else sems_compute[key[1]]
            eng.wait_ge(sem, val)
        ins = op.fn()
        if op.is_dma:
            ins.then_inc(sems_dma[op.sem], 16)
        elif op.need_inc:
            ins.then_inc(sems_compute[e], 1)
    if e == "sp":
        need = {}
        for d in sch.final_dma:
            need[d.sem] = max(need.get(d.sem, 0), d.sem_val)
        for si, val in need.items():
            eng.wait_ge(sems_dma[si], val)


_NC_CACHE = {}


def _core_inputs(inp, sl, S, nseq):
    f = lambda a: np.ascontiguousarray(np.asarray(a, dtype=np.float32))
    m = {
        "x": f(inp["x"][sl]).reshape(nseq * S, D),
        "c": f(inp["c"][sl]),
        "w_ada": f(inp["w_ada"][0]),
        "b_ada": f(inp["b_ada"][0]).reshape(48, 128),
        "w_in": f(inp["w_in"][0]),
        "lb_logits": f(inp["lb_logits"]).reshape(8, 128),
        "hgrn_norm_w": f(inp["hgrn_norm_w"][0]).reshape(128, 1),
        "lam_q1": f(inp["lam_q1"]), "lam_k1": f(inp["lam_k1"]),
        "lam_q2": f(inp["lam_q2"]), "lam_k2": f(inp["lam_k2"]),
        "diff_norm_w": f(inp["diff_norm_w"][0]).reshape(128, 1),
        "rel_bias": f(inp["rel_bias"]),
        "w_out": f(inp["w_out"][0]),
        "ln1_g": f(inp["ln1_g"]), "ln1_b": f(inp["ln1_b"]),
        "w_gate": f(inp["w_gate"][0]), "w_up": f(inp["w_up"][0]), "w_down": f(inp["w_down"][0]),
        "ln2_g": f(inp["ln2_g"]), "ln2_b": f(inp["ln2_b"]),
    }
    m.update(make_consts())
    return m


def kernel(**inputs):
    x = np.asarray(inputs["x"])
    B, S, _ = x.shape
    ncores = 8
    nseq = B // ncores
    key = (S, nseq)
    if key not in _NC_CACHE:
        _NC_CACHE[key] = build(S=S, NSEQ=nseq, mode="full")
    nc = _NC_CACHE[key]
    in_maps = [_core_inputs(inputs, slice(i * nseq, (i + 1) * nseq), S, nseq) for i in range(ncores)]
    res = run_bass_kernel_spmd(nc, in_maps, core_ids=list(range(ncores)))
    outs = [np.asarray(r["out"]).reshape(nseq, S, D) for r in res.results]
    return np.concatenate(outs, axis=0).astype(np.float32)
```

```python
import math
import numpy as np
import concourse.bass as bass
import concourse.mybir as mybir
from concourse.bass_utils import run_bass_kernel_spmd

F32 = mybir.dt.float32
BF16 = mybir.dt.bfloat16
AF = mybir.ActivationFunctionType
ALU = mybir.AluOpType

D = 1024
DFF = 2816
NF = DFF // 128
INW = 3584
LN_EPS = 1e-5
RMS_EPS = 1e-6
ALPHA = 2.0 ** 0.25
LAM_INIT = 0.8 - 0.6 * math.exp(0.0)
MASKVAL = -240000.0


class Buf:
    __slots__ = ("name", "last_w", "readers")

    def __init__(self, name):
        self.name = name
        self.last_w = None
        self.readers = []


class T:
    __slots__ = ("ap", "bufs")

    def __init__(self, ap, bufs):
        self.ap = ap
        self.bufs = bufs if isinstance(bufs, (list, tuple)) else [bufs]

    def __getitem__(self, idx):
        return T(self.ap[idx], self.bufs)

    def bitcast(self, dt):
        return T(self.ap.bitcast(dt), self.bufs)

    def rearrange(self, pat, **kw):
        return T(self.ap.rearrange(pat, **kw), self.bufs)

    def bc(self, shape):
        return T(self.ap.broadcast_to(shape), self.bufs)

    def on(self, *bufs):
        return T(self.ap, list(bufs))


class Op:
    __slots__ = ("eng", "fn", "deps", "pos", "need_inc", "inc_val", "is_dma", "sem", "sem_val", "name")


ENGS = ["pe", "act", "dve", "pool", "sp"]
POOL_DMA_INFLIGHT = 4
SAME_ENG_WIN = 1 << 30


class Sched:
    def __init__(self, nc):
        self.nc = nc
        self.ops = {e: [] for e in ENGS}
        self.barrier_deps = []
        self.dma_sem_of = {}
        self.dma_sem_cnt = []
        self.dma_last = []
        self.final_dma = []
        self.pool_dma_hist = []

    def add(self, eng, fn, reads, writes, dma_key=None, name=""):
        op = Op()
        op.eng = eng
        op.fn = fn
        op.name = name
        op.pos = len(self.ops[eng])
        op.need_inc = False
        op.inc_val = 0
        op.is_dma = dma_key is not None
        op.sem = None
        op.sem_val = 0
        deps = set(self.barrier_deps)
        rb = []
        for t in reads:
            if isinstance(t, T):
                rb.extend(t.bufs)
        wb = []
        for t in writes:
            wb.extend(t.bufs)
        for b in rb:
            if b.last_w is not None:
                deps.add(b.last_w)
        for b in wb:
            if b.last_w is not None:
                deps.add(b.last_w)
            deps.update(b.readers)
        op.deps = deps
        for b in rb:
            b.readers.append(op)
        for b in wb:
            b.last_w = op
            b.readers = []
        if op.is_dma:
            dma_key = (eng, dma_key)
            if dma_key not in self.dma_sem_of:
                self.dma_sem_of[dma_key] = len(self.dma_sem_cnt)
                self.dma_sem_cnt.append(0)
                self.dma_last.append(None)
            si = self.dma_sem_of[dma_key]
            self.dma_sem_cnt[si] += 1
            op.sem = si
            op.sem_val = 16 * self.dma_sem_cnt[si]
            self.dma_last[si] = op
        self.ops[eng].append(op)
        return op

    def barrier(self):
        deps = []
        for e in ENGS:
            if self.ops[e]:
                deps.append(self.ops[e][-1])
        for op in self.dma_last:
            if op is not None:
                deps.append(op)
        self.barrier_deps = deps

    @staticmethod
    def _a(x):
        return x.ap if isinstance(x, T) else x

    def mm(self, out, lhsT, rhs, start=True, stop=True):
        nc = self.nc
        return self.add("pe", lambda: nc.tensor.matmul(out.ap, lhsT.ap, rhs.ap, start=start, stop=stop),
                        [lhsT, rhs], [out], name="mm")

    def tr(self, out, in_, ident):
        nc = self.nc
        return self.add("pe", lambda: nc.tensor.transpose(out.ap, in_.ap, ident.ap), [in_, ident], [out], name="tr")

    def act(self, out, in_, func, scale=1.0, bias=None, eng="act"):
        nc = self.nc
        a = self._a
        kw = {}
        if bias is not None:
            kw["bias"] = a(bias)
        return self.add("act", lambda: nc.scalar.activation(out.ap, in_.ap, func, scale=a(scale), **kw),
                        [in_, scale, bias], [out], name="act")

    def _e(self, eng):
        return self.nc.vector if eng == "dve" else self.nc.gpsimd

    def ts(self, eng, out, in0, s1, op0, s2=None, op1=None):
        e = self._e(eng)
        a = self._a
        if op1 is None:
            fn = lambda: e.tensor_scalar(out.ap, in0.ap, a(s1), None, op0)
        else:
            fn = lambda: e.tensor_scalar(out.ap, in0.ap, a(s1), a(s2), op0, op1)
        return self.add(eng, fn, [in0, s1, s2], [out], name="ts")

    def tt(self, eng, out, in0, in1, op):
        e = self._e(eng)
        return self.add(eng, lambda: e.tensor_tensor(out.ap, in0.ap, in1.ap, op), [in0, in1], [out], name="tt")

    def stt(self, out, in0, scalar, in1, op0, op1):
        nc = self.nc
        a = self._a
        return self.add("dve", lambda: nc.vector.scalar_tensor_tensor(out.ap, in0.ap, a(scalar), in1.ap, op0, op1),
                        [in0, scalar, in1], [out], name="stt")

    def copy(self, eng, out, in_):
        if eng == "act":
            return self.act(out, in_, AF.Copy)
        e = self._e(eng)
        return self.add(eng, lambda: e.tensor_copy(out.ap, in_.ap), [in_], [out], name="copy")

    def recip(self, out, in_):
        nc = self.nc
        return self.add("dve", lambda: nc.vector.reciprocal(out.ap, in_.ap), [in_], [out], name="recip")

    def memset(self, eng, out, val):
        e = self._e(eng)
        return self.add(eng, lambda: e.memset(out.ap, val), [], [out], name="memset")

    def bn_stats(self, out, in_):
        nc = self.nc
        return self.add("dve", lambda: nc.vector.bn_stats(out.ap, in_.ap), [in_], [out], name="bns")

    def bn_aggr(self, out, in_):
        nc = self.nc
        return self.add("dve", lambda: nc.vector.bn_aggr(out.ap, in_.ap), [in_], [out], name="bna")

    def scan(self, out, d0, d1, init, op0, op1):
        nc = self.nc
        return self.add("dve", lambda: nc.vector.tensor_tensor_scan(out.ap, d0.ap, d1.ap, init, op0, op1),
                        [d0, d1], [out], name="scan")

    def dma(self, q, out, in_, key, final=False, **kw):
        nc = self.nc
        e = {"sp": nc.sync, "pool": nc.gpsimd, "act": nc.scalar}[q]
        op = self.add(q, lambda: e.dma_start(out=out.ap, in_=in_.ap, **kw), [in_], [out], dma_key=key, name="dma")
        if q == "pool":
            hist = self.pool_dma_hist
            if len(hist) >= POOL_DMA_INFLIGHT:
                op.deps.add(hist[-POOL_DMA_INFLIGHT])
            hist.append(op)
        if final:
            self.final_dma.append(op)
        return op

    def emit(self, block_engines, sems_compute, sems_dma):
        for e in ENGS:
            for op in self.ops[e]:
                for d in op.deps:
                    if d.is_dma:
                        continue
                    if d.eng != op.eng:
                        d.need_inc = True
                    elif op.eng != "pe" and (op.pos - d.pos) <= SAME_ENG_WIN:
                        d.need_inc = True
        for e in ENGS:
            cnt = 0
            for op in self.ops[e]:
                if op.is_dma:
                    continue
                if op.need_inc:
                    cnt += 1
                    op.inc_val = cnt
        for e in ENGS:
            eng = block_engines[e]
            waited = {}
            ops = self.ops[e]
            for op in ops:
                need = {}
                for d in op.deps:
                    if d.is_dma:
                        key = ("d", d.sem)
                        val = d.sem_val
                    else:
                        if d.eng == e and (e == "pe" or (op.pos - d.pos) > SAME_ENG_WIN):
                            continue
                        key = ("c", d.eng)
                        val = d.inc_val
                    if val > need.get(key, 0):
                        need[key] = val
                for key, val in need.items():
                    if waited.get(key, 0) >= val:
                        continue
                    waited[key] = val
                    sem = sems_dma[key[1]] if key[0] == "d" else sems_compute[key[1]]
                    eng.wait_ge(sem, val)
                ins = op.fn()
                if op.is_dma:
                    ins.then_inc(sems_dma[op.sem], 16)
                elif op.need_inc:
                    ins.then_inc(sems_compute[e], 1)
            if e == "sp":
                need = {}
                for d in self.final_dma:
                    need[d.sem] = max(need.get(d.sem, 0), d.sem_val)
                for si, val in need.items():
                    eng.wait_ge(sems_dma[si], val)


def _t5_bucket_np(dist):
    dist = np.asarray(dist, dtype=np.int32)
    d = np.maximum(dist, 1).astype(np.float32)
    large = 16 + (np.log(d / np.float32(16.0)) / np.float32(math.log(8.0)) * np.float32(16.0)).astype(np.int32)
    large = np.minimum(large, 31)
    return np.where(dist < 16, dist, large)


def make_consts():
    ident = np.eye(128, dtype=np.float32)
    tri = (np.arange(128)[:, None] <= np.arange(128)[None, :]).astype(np.float32)
    scanmask = np.ones((128, 256), np.float32)
    scanmask[:, 0] = 0.0
    scanmask[:, 128] = 0.0
    ohg = np.zeros((32, 384), np.float32)
    j = np.arange(127, 383)
    ohg[_t5_bucket_np(j - 127), j] = 1.0
    rev = np.ascontiguousarray(ident[::-1])
    return {"c_ident": ident, "c_tri": tri, "c_scanmask": scanmask, "c_ohg": ohg, "c_rev": rev}


def build(S=4096, NSEQ=2, mode="full"):
    nc = bass.Bass("TRN2", target_bir_lowering=False)
    TOK = NSEQ * S
    NST = S // 256
    sch = Sched(nc)

    def din(name, shape, dt=F32):
        return T(nc.dram_tensor(name, list(shape), dt, kind="ExternalInput").ap(), Buf(name))

    x_d = din("x", [TOK, D])
    c_d = din("c", [NSEQ, D])
    wada_d = din("w_ada", [D, 6 * D])
    bada_d = din("b_ada", [48, 128])
    win_d = din("w_in", [D, INW])
    lbl_d = din("lb_logits", [8, 128])
    hnw_d = din("hgrn_norm_w", [128, 1])
    lam_d = [din(n, [1, 64]) for n in ("lam_q1", "lam_k1", "lam_q2", "lam_k2")]
    dnw_d = din("diff_norm_w", [128, 1])
    relb_d = din("rel_bias", [32, 4])
    wout_d = din("w_out", [D, D])
    ln1g_d = din("ln1_g", [1, D])
    ln1b_d = din("ln1_b", [1, D])
    wg_d = din("w_gate", [D, DFF])
    wu_d = din("w_up", [D, DFF])
    wd_d = din("w_down", [DFF, D])
    ln2g_d = din("ln2_g", [1, D])
    ln2b_d = din("ln2_b", [1, D])
    cid_d = din("c_ident", [128, 128])
    ctri_d = din("c_tri", [128, 128])
    csm_d = din("c_scanmask", [128, 256])
    cohg_d = din("c_ohg", [32, 384])
    crev_d = din("c_rev", [128, 128])
    out_d = T(nc.dram_tensor("out", [TOK, D], F32, kind="ExternalOutput").ap(), Buf("out"))
    x1s_d = T(nc.dram_tensor("x1s", [TOK, D], F32, kind="Internal").ap(), Buf("x1s"))
    gscr_d = T(nc.dram_tensor("gscr", [4, 384], F32, kind="Internal").ap(), Buf("gscr"))

    import contextlib
    es = contextlib.ExitStack()
    with es:
        ARENA_W = 52992
        arena = es.enter_context(nc.sbuf_tensor("arena", [128, ARENA_W], F32))
        psum = [es.enter_context(nc.psum_tensor(f"ps{i}", [128, 512], F32)) for i in range(8)]
        PS = [T(psum[i][:, :], Buf(f"ps{i}")) for i in range(8)]

        class Alloc:
            def __init__(self, base, limit):
                self.off = base
                self.limit = limit

            def take(self, name, words, dt=F32, shape=None, buf=None):
                a = self.off
                self.off += (words + 7) // 8 * 8
                assert self.off <= self.limit, (name, self.off, self.limit)
                ap = arena[:, a:a + words]
                if dt == BF16:
                    ap = ap.bitcast(BF16)
                t = T(ap, buf if buf is not None else Buf(name))
                if shape is not None:
                    names = " ".join(f"d{i}" for i in range(len(shape)))
                    kw = {f"d{i}": s for i, s in enumerate(shape)}
                    t = t.rearrange(f"p ({names}) -> p {names}", **kw)
                return t

        AP_ = Alloc(0, ARENA_W)
        ident_f = AP_.take("ident_f", 128, shape=None)
        ident_b = AP_.take("ident_b", 64, BF16)
        ones_b = AP_.take("ones_b", 64, BF16)
        tri_b = AP_.take("tri_b", 64, BF16)
        zeros_f = AP_.take("zeros_f", 128)
        adaT = AP_.take("adaT", 48 * NSEQ, shape=[48, NSEQ])
        scl = AP_.take("scl", 16 * NSEQ, shape=[2, 8, NSEQ])
        smallv = AP_.take("smallv", 64)
        gbc = AP_.take("gbc", D)
        bbc = AP_.take("bbc", D)
        PERSIST_END = AP_.off

        lbT = smallv[:, 0:4]
        hnw = smallv[:, 8:9]
        dnw8 = smallv[:, 9:10]
        neglam = smallv[:, 10:11]
        crel = smallv[:, 11:12]
        lamt = smallv[:, 12:16]

        bank_rr = [0]

        def nextbank(pool):
            b = pool[bank_rr[0] % len(pool)]
            bank_rr[0] += 1
            return PS[b]

        def ln_stats(src, st6, mv, sm):
            sch.bn_stats(st6[:, 0:6], src[:, 0:512])
            sch.bn_stats(st6[:, 6:12], src[:, 512:1024])
            sch.bn_aggr(mv, st6)
            sch.act(sm[:, 2:3], mv[:, 1:2], AF.Ln, scale=1.0, bias=epsln)
            sch.act(sm[:, 0:1], sm[:, 2:3], AF.Exp, scale=-0.5)
            sch.stt(sm[:, 1:2], mv[:, 0:1], -1.0, sm[:, 0:1], ALU.mult, ALU.mult)

        def ln_mod_T(xts, xhat, U, sidx, which, banks, st6, mv, sm):
            pa, pb = banks
            pab = [pa.bitcast(BF16).rearrange("p (k t) -> p k t", k=4),
                   pb.bitcast(BF16).rearrange("p (k t) -> p k t", k=4)]
            for j in range(2):
                ln_stats(xts[j], st6[j], mv[j], sm[j])
                sch.act(xhat[:, j, :], xts[j], AF.Identity, scale=sm[j][:, 0:1], bias=sm[j][:, 1:2])
                for kc in range(8):
                    sch.tr(pab[kc // 4][:, kc % 4, j * 128:(j + 1) * 128], xhat[:, j, kc * 128:(kc + 1) * 128], ident_b)
            shift_base = 0 if which == 0 else 24
            for kc in range(8):
                src = pab[kc // 4][:, kc % 4, :]
                s_ap = scl[:, which, kc, sidx:sidx + 1]
                b_ap = adaT[:, shift_base + kc, sidx:sidx + 1]
                if kc % 2 == 0:
                    sch.act(U[:, kc, :], src, AF.Identity, scale=s_ap, bias=b_ap)
                else:
                    sch.ts("dve", U[:, kc, :], src, s_ap, ALU.mult, b_ap, ALU.add)

        def ln_final(t2, xn, ob, st6, mv, sm):
            ln_stats(t2, st6, mv, sm)
            sch.act(xn, t2, AF.Identity, scale=sm[:, 0:1], bias=sm[:, 1:2])
            sch.tt("pool", ob, xn, gbc, ALU.mult)
            sch.tt("pool", ob, ob, bbc, ALU.add)

        def make_bc_from_ada(dst, chunk0, sidx, plus_one, banks):
            for j in range(8):
                pb = nextbank(banks)
                rep = reptile
                sch.ts("dve", rep, zeros_f, adaT[:, chunk0 + j, sidx:sidx + 1], ALU.add,
                       1.0 if plus_one else 0.0, ALU.add)
                sch.tr(pb[:, 0:128], rep, ident_f)
                sch.copy("dve", dst[:, j * 128:(j + 1) * 128], pb[:, 0:128])

        P0 = Alloc(PERSIST_END, ARENA_W)
        epsln = P0.take("epsln", 8)
        sch.memset("dve", epsln[:, 0:1], LN_EPS)
        sch.memset("dve", epsln[:, 1:2], RMS_EPS)
        sch.memset("dve", epsln[:, 2:3], 1.0)
        epsrms = epsln[:, 1:2]
        onec = epsln[:, 2:3]
        epsln = epsln[:, 0:1]
        reptile = P0.take("reptile", 128)
        WIN_WORDS = 8 * INW // 2
        win = None
        if mode in ("full", "mixer"):
            win = P0.take("win", WIN_WORDS, BF16, shape=[8, INW])
            for kc in range(8):
                sch.dma("pool", win[:, kc, :], win_d[kc * 128:(kc + 1) * 128, :], key=Buf(f"winld{kc}"),
                        max_dma_last_dim=4096)
        stage = P0.take("stage", 384)
        sch.dma("sp", ident_f, cid_d, key=ident_f.bufs[0])
        sch.copy("dve", ident_b, ident_f)
        sch.memset("dve", ones_b, 1.0)
        sch.memset("dve", zeros_f, 0.0)
        sch.dma("sp", stage[:, 0:128], ctri_d, key=stage.bufs[0])
        sch.copy("dve", tri_b, stage[:, 0:128])
        sch.dma("sp", hnw, hnw_d, key=Buf("hnwld"))
        dnw_raw = P0.take("dnw_raw", 8)
        sch.dma("sp", dnw_raw[:, 0:1], dnw_d, key=dnw_raw.bufs[0])
        sch.ts("dve", dnw8, dnw_raw[:, 0:1], 1.0 - LAM_INIT, ALU.mult)
        lamv = P0.take("lamv", 256, shape=[4, 64])
        for i in range(4):
            sch.dma("sp", lamv[:, i, :], T(lam_d[i].ap.partition_broadcast(128), lam_d[i].bufs), key=Buf(f"lam{i}"))
        lprod = P0.take("lprod", 128, shape=[2, 64])
        sch.tt("dve", lprod[:, 0, :], lamv[:, 0, :], lamv[:, 1, :], ALU.mult)
        sch.tt("dve", lprod[:, 1, :], lamv[:, 2, :], lamv[:, 3, :], ALU.mult)
        nc_v = nc.vector
        sch.add("dve", lambda: nc_v.tensor_reduce(lamt[:, 0:2].ap, lprod.ap, mybir.AxisListType.X, ALU.add),
                [lprod], [lamt], name="red")
        sch.act(lamt[:, 2:4], lamt[:, 0:2], AF.Exp)
        sch.tt("dve", neglam, lamt[:, 3:4], lamt[:, 2:3], ALU.subtract)
        sch.ts("dve", neglam, neglam, -LAM_INIT, ALU.add)
        lbrow = P0.take("lbrow", 128)
        sch.dma("sp", lbrow[0:8, :], lbl_d, key=lbrow.bufs[0])
        pb = nextbank([4, 5, 6, 7])
        sch.tr(pb[:, 0:8], lbrow[0:8, :], ident_f[0:8, 0:8])
        lbtmp = P0.take("lbtmp", 8)
        sch.copy("dve", lbtmp[:, 0:8], pb[:, 0:8])
        sch.tt("dve", lbtmp[:, 0:4], lbtmp[:, 0:4], lbtmp[:, 4:8], ALU.subtract)
        sch.act(lbtmp[:, 4:8], lbtmp[:, 0:4], AF.Exp, scale=-1.0)
        sch.ts("dve", lbtmp[:, 4:8], lbtmp[:, 4:8], 1.0, ALU.add)
        sch.recip(lbT, lbtmp[:, 4:8])
        crow = P0.take("crow", D)
        sch.dma("sp", crow[0:NSEQ, :], c_d, key=crow.bufs[0])
        cwork = P0.take("cwork", D)
        sch.act(cwork[0:NSEQ, :], crow[0:NSEQ, :], AF.Exp, scale=-1.0)
        sch.ts("dve", cwork[0:NSEQ, :], cwork[0:NSEQ, :], 1.0, ALU.add)
        sch.recip(cwork[0:NSEQ, :], cwork[0:NSEQ, :])
        sch.tt("dve", crow[0:NSEQ, :], crow[0:NSEQ, :], cwork[0:NSEQ, :], ALU.mult)
        scT = P0.take("scT", 8 * NSEQ, BF16, shape=[8, 2 * NSEQ])[:, :, 0:NSEQ]
        pb = nextbank([4, 5, 6, 7])
        for kc in range(8):
            sch.tr(pb[:, kc * NSEQ:(kc + 1) * NSEQ], crow[0:NSEQ, kc * 128:(kc + 1) * 128], ident_f[0:NSEQ, 0:NSEQ])
        sch.copy("dve", scT, pb[:, 0:8 * NSEQ].rearrange("p (k s) -> p k s", s=NSEQ))
        badaT = P0.take("badaT", 48)
        brow = P0.take("brow", 128)
        sch.dma("sp", brow[0:48, :], bada_d, key=brow.bufs[0])
        pb = nextbank([4, 5, 6, 7])
        sch.tr(pb[:, 0:48], brow[0:48, :], ident_f[0:48, 0:48])
        sch.copy("dve", badaT, pb[:, 0:48])
        wst = [P0.take(f"wada{i}", 8 * D // 2, BF16, shape=[8, D]) for i in range(3)]
        for piece in range(6):
            w = wst[piece % 3]
            sch.dma("pool", w, T(wada_d.ap[:, piece * D:(piece + 1) * D].rearrange("(k p) n -> p k n", p=128), wada_d.bufs),
                    key=w.bufs[0], max_dma_last_dim=4096)
            pb = nextbank([4, 5, 6, 7])
            for j in range(8):
                for kc in range(8):
                    sch.mm(pb[:, j * NSEQ:(j + 1) * NSEQ], w[:, kc, j * 128:(j + 1) * 128], scT[:, kc, :],
                           start=(kc == 0), stop=(kc == 7))
            for s_ in range(NSEQ):
                sch.tt("dve", adaT[:, piece * 8:(piece + 1) * 8, s_],
                       pb[:, 0:8 * NSEQ].rearrange("p (j s) -> p j s", s=NSEQ)[:, :, s_],
                       badaT[:, piece * 8:(piece + 1) * 8], ALU.add)
        sch.ts("dve", scl[:, 0, :, :], adaT[:, 8:16, :], 1.0, ALU.add)
        sch.ts("dve", scl[:, 1, :, :], adaT[:, 32:40, :], 1.0, ALU.add)

        if mode in ("full", "mixer"):
            sch.barrier()
            A1 = Alloc(PERSIST_END, ARENA_W)
            epsln_k = A1.take("eps_keep", 8)
            reptile = A1.take("reptile1", 128)
            A1.take("win_keep", WIN_WORDS)
            wout = A1.take("wout", 8 * D // 2, BF16, shape=[8, D])
            kT = A1.take("kT", 4 * S // 2, BF16, shape=[4, S])
            Vc = A1.take("Vc", (S // 128) * 512 // 2, BF16, shape=[S // 128, 512])
            xt_buf = [Buf("xt0"), Buf("xt1")]
            xtile = A1.take("xt", 2 * D, shape=[2, D])
            xts = [xtile[:, j, :].on(xt_buf[j]) for j in range(2)]
            U = A1.take("U", 8 * 256 // 2, BF16, shape=[8, 256])
            qpad = A1.take("qpad", 4 * 2 * 256 // 2, BF16, shape=[4, 2, 256])
            G = [A1.take(f"G{i}", D) for i in range(5)]
            qhat = A1.take("qhat", 512, BF16, shape=[4, 256])
            khat = A1.take("khat", 512, BF16, shape=[4, 256])
            gate = A1.take("gate", 512, BF16, shape=[4, 256])
            vh = A1.take("vh", 512, BF16, shape=[2, 512])
            Am = A1.take("Am", 256, BF16, shape=[4, 128])
            khT = A1.take("khT", 256, BF16, shape=[4, 128])
            Sbf = A1.take("Sbf", 256, BF16, shape=[4, 128])
            Sst = A1.take("Sst", 512, shape=[4, 128])
            PT = [A1.take(f"PT{i}", 256, BF16, shape=[2, 256]) for i in range(2)]
            biasT = A1.take("biasT", 512, BF16, shape=[4, 256])
            scanmask = A1.take("scanmask", 256)
            smt = A1.take("smt", 64)
            lnsm = [A1.take(f"lnsm{i}", 24) for i in range(4)]
            rlb = A1.take("rlb", 512, shape=[2, 256])
            odb = A1.take("odb", 256)
            sqb = A1.take("sqb", 128, BF16)
            rsb = A1.take("rsb", 256)
            print("phase1 arena words used", A1.off, "of", ARENA_W)

            BK = [6, 2, 3, 7, 0, 1, 4, 5]
            BKB = [6]
            BKP = [7]

            sch.dma("sp", scanmask, csm_d, key=scanmask.bufs[0])
            relb = G[0][0:32, 0:4]
            sch.dma("sp", relb, relb_d, key=Buf("relbld"))
            ohg_c = G[1][0:32, 0:384]
            sch.dma("sp", ohg_c, cohg_d, key=Buf("ohgld"))
            sch.dma("sp", crel[0:4, :], T(relb_d.ap[31:32, :].rearrange("o h -> h o"), relb_d.bufs), key=Buf("crelld"))
            pb = nextbank(BK)
            sch.mm(pb[0:4, 0:384], relb, ohg_c)
            grow = G[2][0:4, 0:384]
            sch.ts("dve", grow, pb[0:4, 0:384], crel[0:4, :], ALU.subtract, 8.0, ALU.mult)
            sch.memset("dve", grow[:, 0:127], MASKVAL)
            sch.dma("sp", gscr_d, grow, key=Buf("gscrst"))
            btf = G[3].rearrange("p (h j) -> p h j", h=4)
            rev_f = G[0][:, 512:640]
            sch.dma("sp", rev_f, crev_d, key=Buf("revld"))
            for h in range(4):
                src = bass.AP(tensor=gscr_d.ap.tensor, offset=h * 384, ap=[[1, 128], [1, 256]])
                sch.dma("sp", btf[:, h, :], T(src, gscr_d.bufs), key=Buf(f"btf{h}"))
                pb = nextbank(BK)
                sch.mm(pb[:, 0:256], rev_f, btf[:, h, :])
                sch.copy("dve", biasT[:, h, :], pb[:, 0:256])
            sch.memset("pool", qpad, 0.0)

            def load_wout(sidx):
                make_bc_from_ada(G[4], 16, sidx, True, BK)
                for kc in range(8):
                    stg = G[kc % 4]
                    sch.dma("sp", stg, wout_d[kc * 128:(kc + 1) * 128, :], key=stg.bufs[0])
                    sch.tt("dve", wout[:, kc, :], stg, G[4], ALU.mult)

            sch.dma("sp", gbc, T(ln1g_d.ap.partition_broadcast(128), ln1g_d.bufs), key=gbc.bufs[0])
            sch.dma("sp", bbc, T(ln1b_d.ap.partition_broadcast(128), ln1b_d.bufs), key=bbc.bufs[0])

            st6 = [lnsm[j][:, 0:12] for j in range(4)]
            mv = [lnsm[j][:, 12:14] for j in range(4)]
            sm = [lnsm[j][:, 16:20] for j in range(4)]
            items = [(sidx, st) for sidx in range(NSEQ) for st in range(NST)]
            NI = len(items)
            xhat = G[4].bitcast(BF16).rearrange("p (j d) -> p j d", j=2)
            uT = G[4].bitcast(BF16).rearrange("p (k t) -> p k t", k=8)
            E, L1, L2, hqs = G[0], G[1], G[2], G[3]
            E4 = E.rearrange("p (h t) -> p h t", h=4)
            L14 = L1.rearrange("p (h t) -> p h t", h=4)
            L24 = L2.rearrange("p (h t) -> p h t", h=4)
            hq4 = hqs.rearrange("p (h t) -> p h t", h=4)

            def A_pre(i):
                sidx, st = items[i]
                tok0 = sidx * S + st * 256
                for j in range(2):
                    sch.dma("sp", xts[j], x_d[tok0 + j * 128: tok0 + (j + 1) * 128, :], key=xt_buf[j])
                    yield
                for j in range(2):
                    ln_stats(xts[j], st6[j], mv[j], sm[j])
                    yield
                    sch.act(xhat[:, j, :], xts[j], AF.Identity, scale=sm[j][:, 0:1], bias=sm[j][:, 1:2])
                    yield

            def A_tr(i):
                sidx, st = items[i]
                pa, pb = PS[0], PS[1]
                bank_rr[0] = 0
                pab = [pa.bitcast(BF16).rearrange("p (k t) -> p k t", k=4),
                       pb.bitcast(BF16).rearrange("p (k t) -> p k t", k=4)]
                for j in range(2):
                    for kc in range(8):
                        sch.tr(pab[kc // 4][:, kc % 4, j * 128:(j + 1) * 128], xhat[:, j, kc * 128:(kc + 1) * 128], ident_b)
                    yield
                for kc in range(8):
                    src = pab[kc // 4][:, kc % 4, :]
                    s_ap = scl[:, 0, kc, sidx:sidx + 1]
                    b_ap = adaT[:, kc, sidx:sidx + 1]
                    if kc % 2 == 0:
                        sch.act(uT[:, kc, :], src, AF.Identity, scale=s_ap, bias=b_ap)
                    else:
                        sch.ts("dve", uT[:, kc, :], src, s_ap, ALU.mult, b_ap, ALU.add)
                    if kc % 2 == 1:
                        yield

            def A_pre_tr(i):
                yield from A_pre(i)
                yield from A_tr(i)

            def A_proj(i):
                sidx, st = items[i]
                q0 = st * 256

                def fm_group(col0, h0):
                    pb = nextbank(BK)
                    pv = pb.rearrange("p (h t) -> p h t", h=2)
                    for hh in range(2):
                        c0 = col0 + (h0 + hh) * 128
                        for kc in range(8):
                            sch.mm(pv[:, hh, :], win[:, kc, c0:c0 + 128], uT[:, kc, :], start=(kc == 0), stop=(kc == 7))
                    return pv

                for h0 in (0, 2):
                    pv = fm_group(1536, h0)
                    tmpg = L24[:, h0:h0 + 2, :]
                    sch.act(tmpg, pv, AF.Exp, scale=-1.0)
                    sch.act(tmpg, tmpg, AF.Ln, scale=1.0, bias=onec)
                    sch.act(tmpg, tmpg, AF.Exp, scale=-1.0)
                    sch.tt("dve", gate[:, h0:h0 + 2, :], pv, tmpg, ALU.mult)
                for h0 in (0, 2):
                    pv = fm_group(512, h0)
                    sch.act(E4[:, h0:h0 + 2, :], pv, AF.Exp, scale=-1.0)
                for h0 in (0, 2):
                    pv = fm_group(0, h0)
                    sch.copy("dve", hq4[:, h0:h0 + 2, :], pv)
                for j in range(2):
                    for (col0, dst) in ((1024, vh[:, j, :]), (3072, Vc[:, st * 2 + j, :])):
                        pb = nextbank(BK)
                        for kc in range(8):
                            sch.mm(pb, uT[:, kc, j * 128:(j + 1) * 128], win[:, kc, col0:col0 + 512],
                                   start=(kc == 0), stop=(kc == 7))
                        sch.copy("act" if col0 == 1024 else "dve", dst, pb)
                for h0 in (0, 2):
                    pv = fm_group(2048, h0)
                    sch.copy("act", qpad[0:64, h0:h0 + 2, 0, :], pv[0:64, :, :])
                    sch.copy("dve", qpad[64:128, h0:h0 + 2, 1, :], pv[64:128, :, :])
                for h0 in (0, 2):
                    pv = fm_group(2560, h0)
                    sch.copy("act", kT[:, h0:h0 + 2, q0:q0 + 256], pv)

            sm8 = smt[:, 0:16].rearrange("p (g two) -> p g two", two=2)
            ebm = smt[:, 16:24]
            d1 = smt[:, 24:32]
            d2 = smt[:, 32:40]

            def B_pre(i):
                sch.act(L2, E, AF.Ln, scale=1.0, bias=onec)
                yield
                for h in range(4):
                    sch.act(L14[:, h, :], E4[:, h, :], AF.Ln, scale=lbT[:, h:h + 1], bias=onec)
                yield
                sch.tt("dve", L1, L1, L2, ALU.subtract)
                yield
                sch.act(E, L1, AF.Exp)
                yield
                sch.act(E, E, AF.Identity, scale=-1.0, bias=onec)
                for h in range(4):
                    sch.scan(L24[:, h, :], scanmask, L14[:, h, :], 0.0, ALU.mult, ALU.add)
                    yield
                b8 = L2.rearrange("p (g t) -> p g t", t=128)
                sch.copy("dve", sm8, b8[:, :, 63:128:64])
                yield
                sch.tt("dve", b8, b8, sm8[:, :, 0:1].bc([128, 8, 128]), ALU.subtract)
                sch.tt("dve", d2, sm8[:, :, 1], sm8[:, :, 0], ALU.subtract)
                yield
                sch.act(L1, L2, AF.Exp)
                yield
                sch.act(L2, L2, AF.Exp, scale=-1.0)
                sch.act(ebm, sm8[:, :, 0], AF.Exp)
                sch.act(d1, sm8[:, :, 1], AF.Exp)
                sch.act(d2, d2, AF.Exp)
                yield
                sch.stt(qhat, hq4, 128.0 ** -0.5, L14, ALU.mult, ALU.mult)
                yield
                sch.tt("dve", khat[:, 0:2, :], E4[:, 0:2, :], L24[:, 0:2, :], ALU.mult)
                sch.tt("pool", khat[:, 2:4, :], E4[:, 2:4, :], L24[:, 2:4, :], ALU.mult)
                yield

            def B_gen(i):
                ohg = G[0].rearrange("p (h t) -> p h t", h=4)
                for cch in range(2):
                    tsl = slice(cch * 128, (cch + 1) * 128)
                    pA = nextbank(BKB)
                    pA4 = pA.rearrange("p (h t) -> p h t", h=4)
                    for h in range(4):
                        sch.mm(pA4[:, h, :], khat[:, h, tsl], qhat[:, h, tsl])
                    yield
                    sch.tt("dve", Am, pA4, tri_b.rearrange("p (o t) -> p o t", o=1).bc([128, 4, 128]), ALU.mult)
                    pK = nextbank(BKB)
                    pK4 = pK[:, 0:256].bitcast(BF16).rearrange("p (h t) -> p h t", h=4)
                    for h in range(4):
                        sch.tr(pK4[:, h, :], khat[:, h, tsl], ident_b)
                    yield
                    sch.copy("dve", khT, pK4)
                    g8 = lambda v: v.rearrange("p (h c) -> p h c", c=2)[:, :, cch:cch + 1].bc([128, 4, 128])
                    sch.tt("dve", Sbf, Sst, g8(ebm), ALU.mult)
                    yield
                    pO = nextbank(BKB)
                    pO4 = pO.rearrange("p (h t) -> p h t", h=4)
                    for h in range(4):
                        sch.mm(pO4[:, h, :], vh[:, cch, h * 128:(h + 1) * 128], Am[:, h, :], start=True, stop=False)
                        sch.mm(pO4[:, h, :], Sbf[:, h, :], qhat[:, h, tsl], start=False, stop=True)
                    yield
                    sch.copy("dve", ohg[:, :, tsl], pO4)
                    pP = nextbank(BKB)
                    pP4 = pP.rearrange("p (h t) -> p h t", h=4)
                    for h in range(4):
                        sch.mm(pP4[:, h, :], khT[:, h, :], vh[:, cch, h * 128:(h + 1) * 128])
                    yield
                    stmp = G[1][:, 0:512].rearrange("p (h t) -> p h t", h=4)
                    stmp2 = G[1][:, 512:1024].rearrange("p (h t) -> p h t", h=4)
                    sch.tt("pool", stmp, Sst, g8(d1), ALU.mult)
                    sch.tt("dve", stmp2, pP4, g8(d2), ALU.mult)
                    yield
                    sch.tt("dve", Sst, stmp, stmp2, ALU.add)
                    yield
                srcf = G[0]
                sq = G[2].bitcast(BF16)[:, 0:1024]
                sch.act(sq, srcf, AF.Square)
                yield
                rs = G[3]
                for half in range(2):
                    pS = nextbank(BKB)
                    sch.mm(pS, ones_b, sq[:, half * 512:(half + 1) * 512])
                    yield
                    sch.act(rs[:, half * 512:(half + 1) * 512], pS, AF.Ln, scale=1.0 / 128.0, bias=epsrms)
                    yield
                sch.act(rs, rs, AF.Exp, scale=-0.5)
                yield
                sch.tt("dve", rs, rs, srcf, ALU.mult)
                yield
                rs4 = rs.rearrange("p (h t) -> p h t", h=4)
                sch.stt(U[:, 0:4, :], rs4, hnw, gate, ALU.mult, ALU.mult)
                yield

            def C_attn(i, gens):
                sidx, st = items[i]
                q0 = st * 256
                qt0 = q0 // 128
                nkt = qt0 + 2
                live = list(gens)
                post_gen = [None]

                def bg():
                    for g in list(live):
                        try:
                            next(g)
                        except StopIteration:
                            live.remove(g)

                for h in range(4):
                    pO = PS[2 + 2 * (h % 2)]
                    pL = PS[3 + 2 * (h % 2)]
                    pOv = pO.rearrange("p (m t) -> p m t", m=2)
                    pLv = pL.rearrange("p (m t) -> p m t", m=2)

                    def qk(kt):
                        pS = PS[kt % 2]
                        pSv = pS.rearrange("p (m t) -> p m t", m=2)
                        c0 = 128 if kt == qt0 + 1 else 0
                        if kt == qt0 - 1:
                            win_q, win_b = (0, 128), (128, 256)
                        elif kt == qt0:
                            win_q, win_b = (0, 256), (0, 256)
                        elif kt == qt0 + 1:
                            win_q, win_b = (128, 256), (0, 128)
                        else:
                            win_q = None
                        for m in range(2):
                            sch.mm(pSv[:, m, c0:256], kT[:, h, kt * 128:(kt + 1) * 128], qpad[:, h, m, c0:256],
                                   start=True, stop=(win_q is None))
                            if win_q is not None:
                                sch.mm(pSv[:, m, win_q[0]:win_q[1]], ident_b, biasT[:, h, win_b[0]:win_b[1]],
                                       start=False, stop=True)
                        pt = PT[kt % 2]
                        sch.act(pt[:, :, c0:256], pSv[:, :, c0:256], AF.Exp, scale=0.125)
                        return c0

                    def pv(kt, c0):
                        pt = PT[kt % 2]
                        if c0 == 0:
                            pt2 = pt.rearrange("p m t -> p (m t)")
                            sch.mm(pO, Vc[:, kt, h * 128:(h + 1) * 128], pt2, start=(kt == 0), stop=(kt == nkt - 1))
                            sch.mm(pL, ones_b, pt2, start=(kt == 0), stop=(kt == nkt - 1))
                        else:
                            for m in range(2):
                                sch.mm(pOv[:, m, c0:256], Vc[:, kt, h * 128:(h + 1) * 128], pt[:, m, c0:256],
                                       start=(kt == 0), stop=(kt == nkt - 1))
                            for m in range(2):
                                sch.mm(pLv[:, m, c0:256], ones_b, pt[:, m, c0:256],
                                       start=(kt == 0), stop=(kt == nkt - 1))

                    c0s = {0: qk(0)}
                    for kt in range(nkt):
                        if kt + 1 < nkt:
                            c0s[kt + 1] = qk(kt + 1)
                        pv(kt, c0s[kt])
                        bg()
                    if post_gen[0] is not None:
                        for _ in post_gen[0]:
                            pass
                        if post_gen[0] in live:
                            live.remove(post_gen[0])

                    def post(h=h, pOv=pOv, pLv=pLv):
                        sch.act(rlb, pLv, AF.Ln)
                        yield
                        sch.act(rlb, rlb, AF.Exp, scale=-1.0)
                        yield
                        sch.tt("dve", rlb, pOv, rlb, ALU.mult)
                        yield
                        sch.stt(odb, rlb[:, 1, :], neglam, rlb[:, 0, :], ALU.mult, ALU.add)
                        yield
                        sch.tt("pool", sqb, odb, odb, ALU.mult)
                        yield
                        yield
                        pS = nextbank(BKP)
                        sch.mm(pS[:, 0:256], ones_b, sqb)
                        yield
                        sch.act(rsb, pS[:, 0:256], AF.Ln, scale=1.0 / 128.0, bias=epsrms)
                        yield
                        sch.act(rsb, rsb, AF.Exp, scale=-0.5)
                        yield
                        sch.tt("dve", rsb, rsb, odb, ALU.mult)
                        yield
                        sch.ts("dve", U[:, 4 + h, :], rsb, dnw8, ALU.mult)

                    post_gen[0] = post()
                    live.insert(0, post_gen[0])
                while live:
                    bg()

            def D_ld(i, j):
                sidx, st = items[i]
                tok0 = sidx * S + st * 256
                t2s = xts
                sch.dma("sp", t2s[j], x_d[tok0 + j * 128: tok0 + (j + 1) * 128, :], key=xt_buf[j])

            def D_mm(i):
                sidx, st = items[i]
                tok0 = sidx * S + st * 256
                t2s = xts
                for j in range(2):
                    pM = [nextbank(BK), nextbank(BK)]
                    for n in range(2):
                        for kc in range(8):
                            sch.mm(pM[n], U[:, kc, j * 128:(j + 1) * 128], wout[:, kc, n * 512:(n + 1) * 512],
                                   start=(kc == 0), stop=(kc == 7))
                    t2 = t2s[j]
                    for n in range(2):
                        sch.stt(t2[:, n * 512:(n + 1) * 512], t2[:, n * 512:(n + 1) * 512], ALPHA, pM[n],
                                ALU.mult, ALU.add)

            def D_fin(i):
                sidx, st = items[i]
                tok0 = sidx * S + st * 256
                t2s = xts
                for j in range(2):
                    t2 = t2s[j]
                    ln_final(t2, t2, t2, st6[2 + j], mv[2 + j], sm[2 + j])
                    dst = x1s_d if mode == "full" else out_d
                    sch.dma("pool", dst[tok0 + j * 128: tok0 + (j + 1) * 128, :], t2, key=xt_buf[j],
                            final=(mode != "full"))

            def drain(g):
                for _ in g:
                    pass

            for i in range(NI):
                sidx, st = items[i]
                if st == 0:
                    load_wout(sidx)
                    sch.memset("pool", Sst, 0.0)
                    drain(A_pre_tr(i))
                    A_proj(i)
                    drain(B_pre(i))
                nxt_same = (i + 1 < NI) and items[i + 1][1] != 0
                def Bfull(i=i):
                    if st != 0:
                        yield from B_pre(i)
                    yield from B_gen(i)
                gens = [Bfull()]
                if nxt_same:
                    gens.append(A_pre(i + 1))
                C_attn(i, gens)
                D_ld(i, 0)
                D_ld(i, 1)
                if nxt_same:
                    drain(A_tr(i + 1))
                    A_proj(i + 1)
                D_mm(i)
                D_fin(i)

        if mode in ("full", "ffn"):
            sch.barrier()
            A2 = Alloc(PERSIST_END, ARENA_W)
            epsln_k = A2.take("eps_keep2", 8)
            reptile = A2.take("reptile2", 128)
            wg = A2.take("wg", 8 * DFF // 2, BF16, shape=[8, DFF])
            wu = A2.take("wu", 8 * DFF // 2, BF16, shape=[8, DFF])
            wd = A2.take("wd", NF * D // 2, BF16, shape=[NF, D])
            x1t = [[A2.take(f"x1t{s}{j}", D) for j in range(2)] for s in range(2)]
            xhat2 = A2.take("xhat2", D, BF16, shape=[2, D])
            U2 = [A2.take(f"U2{i}", 8 * 256 // 2, BF16, shape=[8, 256]) for i in range(2)]
            hT = A2.take("hT", NF * 256 // 2, BF16, shape=[NF, 256])
            hTf = [hT[:, f, :].on(Buf(f"hT{f}")) for f in range(NF)]
            sw = [[A2.take(f"sw{i}{k}", 256) for k in range(2)] for i in range(2)]
            tb = A2.take("tb", D)
            obb = [A2.take(f"obb{i}", D) for i in range(2)]
            gmbc = [A2.take(f"gmbc{i}", D) for i in range(NSEQ)]
            lnsm2 = [A2.take(f"lnsm2{i}", 24) for i in range(4)]
            print("phase2 arena words used", A2.off, "of", ARENA_W)
            BK2 = [0, 1, 2, 3, 4, 5, 6, 7]
            for kc in range(8):
                sch.dma("pool", wg[:, kc, :], wg_d[kc * 128:(kc + 1) * 128, :], key=Buf(f"wgld{kc}"), max_dma_last_dim=4096)
                sch.dma("pool", wu[:, kc, :], wu_d[kc * 128:(kc + 1) * 128, :], key=Buf(f"wuld{kc}"), max_dma_last_dim=4096)
            for f in range(NF):
                sch.dma("pool", wd[:, f, :], wd_d[f * 128:(f + 1) * 128, :], key=Buf(f"wdld{f}"), max_dma_last_dim=4096)
            sch.dma("sp", gbc, T(ln2g_d.ap.partition_broadcast(128), ln2g_d.bufs), key=gbc.bufs[0])
            sch.dma("sp", bbc, T(ln2b_d.ap.partition_broadcast(128), ln2b_d.bufs), key=bbc.bufs[0])
            for sidx in range(NSEQ):
                make_bc_from_ada(gmbc[sidx], 40, sidx, True, BK2)
            src_d = x1s_d if mode == "full" else x_d
            st6 = [lnsm2[j][:, 0:12] for j in range(4)]
            mv = [lnsm2[j][:, 12:14] for j in range(4)]
            sm = [lnsm2[j][:, 16:20] for j in range(4)]
            items = [(sidx, st) for sidx in range(NSEQ) for st in range(NST)]

            def p2_lnA(i):
                sidx, st = items[i]
                tok0 = sidx * S + st * 256
                xts2 = x1t[i % 2]
                for j in range(2):
                    sch.dma("sp", xts2[j], src_d[tok0 + j * 128: tok0 + (j + 1) * 128, :], key=xts2[j].bufs[0])
                for j in range(2):
                    ln_stats(xts2[j], st6[j], mv[j], sm[j])
                    sch.act(xhat2[:, j, :], xts2[j], AF.Identity, scale=sm[j][:, 0:1], bias=sm[j][:, 1:2])

            def p2_trB(i):
                sidx, st = items[i]
                pa, pb = nextbank(BK2), nextbank(BK2)
                pab = [pa.bitcast(BF16).rearrange("p (k t) -> p k t", k=4),
                       pb.bitcast(BF16).rearrange("p (k t) -> p k t", k=4)]
                for j in range(2):
                    for kc in range(8):
                        sch.tr(pab[kc // 4][:, kc % 4, j * 128:(j + 1) * 128], xhat2[:, j, kc * 128:(kc + 1) * 128], ident_b)
                U = U2[i % 2]
                for kc in range(8):
                    src = pab[kc // 4][:, kc % 4, :]
                    s_ap = scl[:, 1, kc, sidx:sidx + 1]
                    b_ap = adaT[:, 24 + kc, sidx:sidx + 1]
                    if kc % 2 == 0:
                        sch.act(U[:, kc, :], src, AF.Identity, scale=s_ap, bias=b_ap)
                    else:
                        sch.ts("dve", U[:, kc, :], src, s_ap, ALU.mult, b_ap, ALU.add)

            def p2_gu(i):
                U = U2[i % 2]
                for f in range(NF):
                    pb = nextbank(BK2)
                    pv = pb.rearrange("p (g t) -> p g t", g=2)
                    for gi, w in enumerate((wg, wu)):
                        for kc in range(8):
                            sch.mm(pv[:, gi, :], w[:, kc, f * 128:(f + 1) * 128], U[:, kc, :],
                                   start=(kc == 0), stop=(kc == 7))
                    ee, rr = sw[f % 2]
                    sch.act(ee, pv[:, 0, :], AF.Exp, scale=-1.0)
                    sch.act(ee, ee, AF.Ln, scale=1.0, bias=onec)
                    sch.act(rr, ee, AF.Exp, scale=-1.0)
                    sch.tt("dve", rr, pv[:, 0, :], rr, ALU.mult)
                    sch.tt("dve", hTf[f], rr, pv[:, 1, :], ALU.mult)

            def p2_downmm(i, j):
                pY = [nextbank(BK2), nextbank(BK2)]
                for n in range(2):
                    for f in range(NF):
                        sch.mm(pY[n], hTf[f][:, j * 128:(j + 1) * 128], wd[:, f, n * 512:(n + 1) * 512],
                               start=(f == 0), stop=(f == NF - 1))
                return pY

            def p2_fin(i, j, pY):
                sidx, st = items[i]
                tok0 = sidx * S + st * 256
                xts2 = x1t[i % 2]
                for n in range(2):
                    sch.tt("dve", tb[:, n * 512:(n + 1) * 512], pY[n], gmbc[sidx][:, n * 512:(n + 1) * 512], ALU.mult)
                sch.stt(tb, xts2[j], ALPHA, tb, ALU.mult, ALU.add)
                ob = obb[j]
                ln_final(tb, ob, ob, st6[2 + j], mv[2 + j], sm[2 + j])
                sch.dma("pool", out_d[tok0 + j * 128: tok0 + (j + 1) * 128, :], ob, key=ob.bufs[0], final=True)

            NI = len(items)
            p2_lnA(0)
            p2_trB(0)
            for i in range(NI):
                p2_gu(i)
                if i + 1 < NI:
                    p2_lnA(i + 1)
                pY0 = p2_downmm(i, 0)
                if i + 1 < NI:
                    p2_trB(i + 1)
                pY1 = p2_downmm(i, 1)
                p2_fin(i, 0, pY0)
                p2_fin(i, 1, pY1)

        nsem_dma = len(sch.dma_sem_cnt)
        print("ops:", {e: len(sch.ops[e]) for e in ENGS}, "dma sems:", nsem_dma)
        sems_c = {e: es.enter_context(nc.semaphore(f"sc_{e}")) for e in ENGS}
        sems_d = [es.enter_context(nc.semaphore(f"sd_{i}")) for i in range(nsem_dma)]
        block = es.enter_context(nc.Block())
        engs = {}

        @block.tensor
        def _(e):
            engs["pe"] = e
            sch.emit_one = None
            _emit_engine(sch, "pe", e, sems_c, sems_d)

        @block.scalar
        def _(e):
            _emit_engine(sch, "act", e, sems_c, sems_d)

        @block.vector
        def _(e):
            _emit_engine(sch, "dve", e, sems_c, sems_d)

        @block.gpsimd
        def _(e):
            _emit_engine(sch, "pool", e, sems_c, sems_d)

        @block.sync
        def _(e):
            _emit_engine(sch, "sp", e, sems_c, sems_d)
    return nc


_prepared = set()


def _prepare(sch):
    if id(sch) in _prepared:
        return
    _prepared.add(id(sch))
    for e in ENGS:
        for op in sch.ops[e]:
            for d in op.deps:
                if d.is_dma:
                    continue
                if d.eng != op.eng:
                    d.need_inc = True
                elif op.eng != "pe" and (op.pos - d.pos) <= SAME_ENG_WIN:
                    d.need_inc = True
    for e in ENGS:
        cnt = 0
        for op in sch.ops[e]:
            if op.is_dma:
                continue
            if op.need_inc:
                cnt += 1
                op.inc_val = cnt


def _emit_engine(sch, e, eng, sems_compute, sems_dma):
    _prepare(sch)
    waited = {}
    for op in sch.ops[e]:
        need = {}
        for d in op.deps:
            if d.is_dma:
                key = ("d", d.sem)
                val = d.sem_val
            else:
                if d.eng == e and (e == "pe" or (op.pos - d.pos) > SAME_ENG_WIN):
                    continue
                key = ("c", d.eng)
                val = d.inc_val
            if val > need.get(key, 0):
                need[key] = val
        for key, val in need.items():
            if waited.get(key, 0) >= val:
                continue
            waited[key] = val
            sem = sems_dma[key[1]] if key[0] == "d" else sems_compute[key[1]]
            eng.wait_ge(sem, val)
        ins = op.fn()
        if op.is_dma:
            ins.then_inc(sems_dma[op.sem], 16)
        elif op.need_inc:
            ins.then_inc(sems_compute[e], 1)
    if e == "sp":
        need = {}
        for d in sch.final_dma:
            need[d.sem] = max(need.get(d.sem, 0), d.sem_val)
        for si, val in need.items():
            eng.wait_ge(sems_dma[si], val)


_NC_CACHE = {}


def _core_inputs(inp, sl, S, nseq):
    f = lambda a: np.ascontiguousarray(np.asarray(a, dtype=np.float32))
    m = {
        "x": f(inp["x"][sl]).reshape(nseq * S, D),
        "c": f(inp["c"][sl]),
        "w_ada": f(inp["w_ada"][0]),
        "b_ada": f(inp["b_ada"][0]).reshape(48, 128),
        "w_in": f(inp["w_in"][0]),
        "lb_logits": f(inp["lb_logits"]).reshape(8, 128),
        "hgrn_norm_w": f(inp["hgrn_norm_w"][0]).reshape(128, 1),
        "lam_q1": f(inp["lam_q1"]), "lam_k1": f(inp["lam_k1"]),
        "lam_q2": f(inp["lam_q2"]), "lam_k2": f(inp["lam_k2"]),
        "diff_norm_w": f(inp["diff_norm_w"][0]).reshape(128, 1),
        "rel_bias": f(inp["rel_bias"]),
        "w_out": f(inp["w_out"][0]),
        "ln1_g": f(inp["ln1_g"]), "ln1_b": f(inp["ln1_b"]),
        "w_gate": f(inp["w_gate"][0]), "w_up": f(inp["w_up"][0]), "w_down": f(inp["w_down"][0]),
        "ln2_g": f(inp["ln2_g"]), "ln2_b": f(inp["ln2_b"]),
    }
    m.update(make_consts())
    return m


def kernel(**inputs):
    x = np.asarray(inputs["x"])
    B, S, _ = x.shape
    ncores = 8
    nseq = B // ncores
    key = (S, nseq)
    if key not in _NC_CACHE:
        _NC_CACHE[key] = build(S=S, NSEQ=nseq, mode="full")
    nc = _NC_CACHE[key]
    in_maps = [_core_inputs(inputs, slice(i * nseq, (i + 1) * nseq), S, nseq) for i in range(ncores)]
    res = run_bass_kernel_spmd(nc, in_maps, core_ids=list(range(ncores)))
    outs = [np.asarray(r["out"]).reshape(nseq, S, D) for r in res.results]
    return np.concatenate(outs, axis=0).astype(np.float32)
```
